# Optimizing a Trainium2 kernel written in Bass

```python
import jax, jax.numpy as jnp
from jax import lax
import numpy as np

D_MODEL = 4096
BATCH = 4
SEQ = 4096
DEPTH = 4

CTX_LEN = 256
GRID_W = 64
EPS = 1e-6
ROPE_BASE = 10000.0
ATTN_BLOCK = 128
CHUNK = 64

MLA_V = 128
MLA_NOPE = 128
MLA_ROPE = 64
MLA_QK = MLA_NOPE + MLA_ROPE
MLA_HEADS = (D_MODEL // 2) // MLA_V
MLA_W = MLA_HEADS * MLA_V
Q_LORA = 1536
KV_LORA = 512
HG_DK = 128
HG_DV = 128
HG_HEADS = (D_MODEL // 2) // HG_DV
HG_K = HG_HEADS * HG_DK
HG_W = HG_HEADS * HG_DV
RET_DK = 256
RET_DV = 2 * RET_DK
RET_HEADS = D_MODEL // RET_DK
RET_K = RET_HEADS * RET_DK
RET_W = RET_HEADS * RET_DV

N_EVEN = (DEPTH + 1) // 2
N_ODD = DEPTH // 2

E_CQ = 0
E_CKV = E_CQ + Q_LORA
E_KPE = E_CKV + KV_LORA
E_GM = E_KPE + MLA_ROPE
E_QH = E_GM + MLA_W
E_FF = E_QH + HG_K
E_FB = E_FF + HG_K
E_IH = E_FB + HG_K
E_GH = E_IH + HG_W
E_END = E_GH + HG_W
EVEN_SPLIT = (E_CKV, E_KPE, E_GM, E_QH, E_FF, E_FB, E_IH, E_GH)
O_K = RET_K
O_V = 2 * RET_K
O_G = 2 * RET_K + RET_W
O_END = O_G + RET_W

kernel_name = "hybrid_mla_hgrn2_retention_dit"


def rms_norm(x, gain=None):
    xf = x.astype(jnp.float32)
    y = xf * lax.rsqrt(jnp.mean(xf * xf, axis=-1, keepdims=True) + EPS)
    if gain is not None:
        y = y * gain.astype(jnp.float32)
    return y.astype(x.dtype)


def adaln(cond, w_mod, b_mod):
    m = (jax.nn.silu(cond) @ w_mod + b_mod)[..., None, :]
    shift, scale, gate = jnp.split(m, 3, axis=-1)
    return shift, scale, gate


def ada_in(x, gain, mod):
    shift, scale, _ = mod
    return rms_norm(x, gain) * (1 + scale) + shift


def grid_positions(n_tok):
    rows = n_tok // GRID_W
    row = jnp.repeat(jnp.arange(rows), GRID_W).astype(jnp.float32)
    col = jnp.tile(jnp.arange(GRID_W), rows).astype(jnp.float32)
    return row, col


def rotate(x, ang):
    cos = jnp.cos(ang)[:, None, :].astype(x.dtype)
    sin = jnp.sin(ang)[:, None, :].astype(x.dtype)
    x1, x2 = jnp.split(x, 2, axis=-1)
    return jnp.concatenate([x1 * cos - x2 * sin, x1 * sin + x2 * cos], axis=-1)


def rope_2d(x, row, col):
    half = x.shape[-1] // 2
    inv = ROPE_BASE ** (-jnp.arange(0, half, 2, dtype=jnp.float32) / half)
    return jnp.concatenate([rotate(x[..., :half], row[:, None] * inv),
                            rotate(x[..., half:], col[:, None] * inv)], axis=-1)


def rope_retnet(x, pos):
    inv = 1.0 / (ROPE_BASE ** jnp.linspace(0.0, 1.0, x.shape[-1] // 2, dtype=jnp.float32))
    return rotate(x, pos[:, None] * inv)


def to_heads(a, n_heads):
    bn, t, _ = a.shape
    return a.reshape(bn, t, n_heads, -1).transpose(0, 2, 1, 3)


def rope_part(t, pos):
    row, col = pos
    return jnp.concatenate([t[..., :MLA_NOPE], rope_2d(t[..., MLA_NOPE:], row, col)], axis=-1)


def mla_q(cq, q_a_gain, w_uq, q_gain, pos):
    bn, t = cq.shape[:2]
    q = (rms_norm(cq, q_a_gain) @ w_uq).reshape(bn, t, MLA_HEADS, MLA_QK)
    q = rms_norm(q, q_gain)
    return q if pos is None else rope_part(q, pos)


def mla_kv(ckv, kpe, kv_a_gain, w_ukv, k_gain, pos):
    bn, t = ckv.shape[:2]
    kv = (rms_norm(ckv, kv_a_gain) @ w_ukv).reshape(bn, t, MLA_HEADS, MLA_NOPE + MLA_V)
    k = jnp.concatenate([kv[..., :MLA_NOPE],
                         jnp.broadcast_to(kpe[:, :, None, :], (bn, t, MLA_HEADS, MLA_ROPE))], axis=-1)
    k = rms_norm(k, k_gain)
    if pos is not None:
        k = rope_part(k, pos)
    return k, kv[..., MLA_NOPE:]


def attend(q, k, v):
    s = jnp.einsum('bqhd,bkhd->bhqk', q, k).astype(jnp.float32) * (MLA_QK ** -0.5)
    p = jax.nn.softmax(s, axis=-1).astype(v.dtype)
    return jnp.einsum('bhqk,bkhd->bqhd', p, v)


def blocked_attend(q, k, v):
    bn, t, h, d = q.shape
    n_blk = t // ATTN_BLOCK
    qb = q.reshape(bn, n_blk, ATTN_BLOCK, h, d).transpose(1, 0, 2, 3, 4)
    ob = lax.map(lambda qq: attend(qq, k, v), qb)
    return ob.transpose(1, 0, 2, 3, 4).reshape(bn, t, h, -1)


def chunk_scan(q, k, v, logf, s0):
    out_dtype = v.dtype
    f32 = jnp.float32
    q, k, v, logf, s0 = (a.astype(f32) for a in (q, k, v, logf, s0))
    bn, h, t, _ = q.shape
    n_chunk = t // CHUNK

    def to_chunks(a):
        return a.reshape(bn, h, n_chunk, CHUNK, a.shape[-1]).transpose(2, 0, 1, 3, 4)

    mask = jnp.tril(jnp.ones((CHUNK, CHUNK), dtype=bool))[:, :, None]

    def step(state, inp):
        qc, kc, vc, gc = inp
        b = jnp.cumsum(gc, axis=2)
        b_last = b[:, :, -1:, :]
        rel = b[:, :, :, None, :] - b[:, :, None, :, :]
        dec = jnp.exp(jnp.where(mask, rel, -jnp.inf))
        if gc.shape[-1] == 1:
            a = jnp.einsum('bhtd,bhsd->bhts', qc, kc) * dec[..., 0]
        else:
            a = jnp.einsum('bhtd,bhsd,bhtsd->bhts', qc, kc, dec)
        o = (jnp.einsum('bhts,bhsv->bhtv', a, vc)
             + jnp.einsum('bhtd,bhdv->bhtv', qc * jnp.exp(b), state))
        state = (jnp.exp(b_last[:, :, 0, :, None]) * state
                 + jnp.einsum('bhsd,bhsv->bhdv', kc * jnp.exp(b_last - b), vc))
        return state, o

    s_fin, o = lax.scan(step, s0, (to_chunks(q), to_chunks(k), to_chunks(v), to_chunks(logf)))
    o = o.transpose(1, 2, 0, 3, 4).reshape(bn, h, t, -1)
    return o.astype(out_dtype), s_fin


def final_state(k, v, logf):
    f32 = jnp.float32
    b = jnp.cumsum(logf.astype(f32), axis=2)
    return jnp.einsum('bhsd,bhsv->bhdv', k.astype(f32) * jnp.exp(b[:, :, -1:] - b), v.astype(f32))


def flip_t(a):
    return jnp.flip(a, axis=2)


def bidir_scan(q, k_f, k_b, v, lg_f, lg_b, s0_f, s0_b):
    o_f, s_f = chunk_scan(q, k_f, v, lg_f, s0_f)
    o_b, s_b = chunk_scan(flip_t(q), flip_t(k_b), flip_t(v), flip_t(lg_b), s0_b)
    return o_f + flip_t(o_b), s_f, s_b


def bidir_final_states(k_f, k_b, v, lg_f, lg_b):
    s_f = final_state(k_f, v, lg_f)
    s_b = final_state(flip_t(k_b), flip_t(v), flip_t(lg_b))
    return s_f, s_b


def hgrn_gates(f, lb):
    ff = f.astype(jnp.float32)
    log_g = jnp.logaddexp(jnp.log(lb), jnp.log1p(-lb) + jax.nn.log_sigmoid(ff))
    key = (1.0 - lb) * jax.nn.sigmoid(-ff)
    return to_heads(key, HG_HEADS), to_heads(log_g, HG_HEADS)


def even_merge(att, g_m, o_h, g_h, hg_gain):
    bn, t = att.shape[:2]
    a = att.reshape(bn, t, MLA_W) * jax.nn.silu(g_m)
    o = rms_norm(o_h.transpose(0, 2, 1, 3), hg_gain).reshape(bn, t, HG_W) * jax.nn.silu(g_h)
    return jnp.concatenate([a, o], axis=-1)


def ret_merge(o, g):
    bn, _, t, _ = o.shape
    return rms_norm(o.transpose(0, 2, 1, 3)).reshape(bn, t, RET_W) * jax.nn.silu(g)


def even_layer(x, xc, mod_lat, mod_ctx, gain, w_in, q_a_gain, kv_a_gain, w_uq, w_ukv,
               q_gain, k_gain, lb, hg_gain, w_out, pos, ctx_out):
    bn = x.shape[0]
    zeros = jnp.zeros((bn, HG_HEADS, HG_DK, HG_DV), jnp.float32)
    hc = ada_in(xc, gain, mod_ctx)
    if ctx_out:
        cq, ckv, kpe, gm, qh, ff, fb, ih, gh = jnp.split(hc @ w_in, EVEN_SPLIT, axis=-1)
        kc, vc = mla_kv(ckv, kpe, kv_a_gain, w_ukv, k_gain, None)
        att_c = attend(mla_q(cq, q_a_gain, w_uq, q_gain, None), kc, vc)
        kf, lgf = hgrn_gates(ff, lb[0])
        kb, lgb = hgrn_gates(fb, lb[1])
        o_c, s_f, s_b = bidir_scan(to_heads(qh, HG_HEADS), kf, kb, to_heads(ih, HG_HEADS),
                                   lgf, lgb, zeros, zeros)
        xc = xc + mod_ctx[2] * (even_merge(att_c, gm, o_c, gh, hg_gain) @ w_out)
    else:
        ckv, kpe = jnp.split(hc @ w_in[:, E_CKV:E_GM], [KV_LORA], axis=-1)
        ff, fb, ih = jnp.split(hc @ w_in[:, E_FF:E_GH], [HG_K, 2 * HG_K], axis=-1)
        kc, vc = mla_kv(ckv, kpe, kv_a_gain, w_ukv, k_gain, None)
        kf, lgf = hgrn_gates(ff, lb[0])
        kb, lgb = hgrn_gates(fb, lb[1])
        s_f, s_b = bidir_final_states(kf, kb, to_heads(ih, HG_HEADS), lgf, lgb)
        xc = None
    h = ada_in(x, gain, mod_lat)
    cq, ckv, kpe, gm, qh, ff, fb, ih, gh = jnp.split(h @ w_in, EVEN_SPLIT, axis=-1)
    q = mla_q(cq, q_a_gain, w_uq, q_gain, pos)
    k, v = mla_kv(ckv, kpe, kv_a_gain, w_ukv, k_gain, pos)
    att = blocked_attend(q, jnp.concatenate([k, kc], axis=1), jnp.concatenate([v, vc], axis=1))
    kf, lgf = hgrn_gates(ff, lb[0])
    kb, lgb = hgrn_gates(fb, lb[1])
    o_h, _, _ = bidir_scan(to_heads(qh, HG_HEADS), kf, kb, to_heads(ih, HG_HEADS), lgf, lgb, s_f, s_b)
    x = x + mod_lat[2] * (even_merge(att, gm, o_h, gh, hg_gain) @ w_out)
    return x, xc


def odd_layer(x, xc, mod_lat, mod_ctx, gain, w_in, decay_logit, w_out, tpos, ctx_out):
    bn = x.shape[0]
    log_gamma = jax.nn.log_sigmoid(decay_logit.astype(jnp.float32))

    def decays(t):
        return (jnp.broadcast_to(log_gamma[0][None, :, None, None], (bn, RET_HEADS, t, 1)),
                jnp.broadcast_to(log_gamma[1][None, :, None, None], (bn, RET_HEADS, t, 1)))

    zeros = jnp.zeros((bn, RET_HEADS, RET_DK, RET_DV), jnp.float32)
    k_scale = RET_DK ** -0.5
    hc = ada_in(xc, gain, mod_ctx)
    t_c = xc.shape[1]
    lgf_c, lgb_c = decays(t_c)
    if ctx_out:
        q, k, v, g = jnp.split(hc @ w_in, [O_K, O_V, O_G], axis=-1)
        kh = to_heads(k, RET_HEADS) * k_scale
        o_c, s_f, s_b = bidir_scan(to_heads(q, RET_HEADS), kh, kh, to_heads(v, RET_HEADS),
                                   lgf_c, lgb_c, zeros, zeros)
        xc = xc + mod_ctx[2] * (ret_merge(o_c, g) @ w_out)
    else:
        k, v = jnp.split(hc @ w_in[:, O_K:O_G], [RET_K], axis=-1)
        kh = to_heads(k, RET_HEADS) * k_scale
        s_f, s_b = bidir_final_states(kh, kh, to_heads(v, RET_HEADS), lgf_c, lgb_c)
        xc = None
    h = ada_in(x, gain, mod_lat)
    t = x.shape[1]
    q, k, v, g = jnp.split(h @ w_in, [O_K, O_V, O_G], axis=-1)
    qh = rope_retnet(q.reshape(bn, t, RET_HEADS, RET_DK), tpos).transpose(0, 2, 1, 3)
    kh = (rope_retnet(k.reshape(bn, t, RET_HEADS, RET_DK), tpos) * k_scale).transpose(0, 2, 1, 3)
    lgf, lgb = decays(t)
    o, _, _ = bidir_scan(qh, kh, kh, to_heads(v, RET_HEADS), lgf, lgb, s_f, s_b)
    x = x + mod_lat[2] * (ret_merge(o, g) @ w_out)
    return x, xc


def setup_inputs(seed: int = 0) -> dict:
    key = jax.random.key(seed)
    ks = jax.random.split(key, 24)
    f32 = jnp.float32

    def nrm(k, shape, scale):
        return jax.random.normal(k, shape, f32) * scale

    gam = 1.0 - 2.0 ** (-5.0 - jnp.arange(RET_HEADS, dtype=f32))
    ret_logit = jnp.log(gam) - jnp.log1p(-gam)
    return {
        "x": nrm(ks[0], (BATCH, SEQ, D_MODEL), 1.0),
        "c": nrm(ks[1], (BATCH, D_MODEL), 1.0),
        "ctx": nrm(ks[2], (BATCH, CTX_LEN, D_MODEL), 1.0),
        "c_ctx": nrm(ks[3], (D_MODEL,), 1.0),
        "mod_w": nrm(ks[4], (DEPTH, D_MODEL, 3 * D_MODEL), D_MODEL ** -0.5),
        "mod_b": nrm(ks[5], (DEPTH, 3 * D_MODEL), 0.02),
        "norm_gain": 1.0 + nrm(ks[6], (DEPTH, D_MODEL), 0.02),
        "even_w_in": nrm(ks[7], (N_EVEN, D_MODEL, E_END), D_MODEL ** -0.5),
        "q_a_gain": 1.0 + nrm(ks[8], (N_EVEN, Q_LORA), 0.02),
        "kv_a_gain": 1.0 + nrm(ks[9], (N_EVEN, KV_LORA), 0.02),
        "w_uq": nrm(ks[10], (N_EVEN, Q_LORA, MLA_HEADS * MLA_QK), Q_LORA ** -0.5),
        "w_ukv": nrm(ks[11], (N_EVEN, KV_LORA, MLA_HEADS * (MLA_NOPE + MLA_V)), KV_LORA ** -0.5),
        "q_norm_gain": 1.0 + nrm(ks[12], (N_EVEN, MLA_QK), 0.02),
        "k_norm_gain": 1.0 + nrm(ks[13], (N_EVEN, MLA_QK), 0.02),
        "hg_lb": nrm(ks[14], (N_EVEN, 2, HG_K), 1.0),
        "hg_norm_gain": 1.0 + nrm(ks[15], (N_EVEN, HG_DV), 0.02),
        "even_w_out": nrm(ks[16], (N_EVEN, MLA_W + HG_W, D_MODEL), (MLA_W + HG_W) ** -0.5),
        "odd_w_in": nrm(ks[17], (N_ODD, D_MODEL, O_END), D_MODEL ** -0.5),
        "ret_decay": ret_logit + nrm(ks[18], (N_ODD, 2, RET_HEADS), 0.1),
        "odd_w_out": nrm(ks[19], (N_ODD, RET_W, D_MODEL), RET_W ** -0.5),
    }


def reference(x, c, ctx, c_ctx, mod_w, mod_b, norm_gain, even_w_in, q_a_gain, kv_a_gain, w_uq,
              w_ukv, q_norm_gain, k_norm_gain, hg_lb, hg_norm_gain, even_w_out, odd_w_in,
              ret_decay, odd_w_out):
    n_tok = x.shape[1]
    row, col = grid_positions(n_tok)
    tpos = jnp.arange(n_tok, dtype=jnp.float32)
    p = jax.nn.softmax(hg_lb.astype(jnp.float32), axis=0)
    lower_bounds = jnp.cumsum(p, axis=0) - p[0]
    xc = ctx
    for l in range(DEPTH):
        i = l // 2
        mod_lat = adaln(c, mod_w[l], mod_b[l])
        mod_ctx = adaln(c_ctx, mod_w[l], mod_b[l])
        ctx_out = l < DEPTH - 1
        if l % 2 == 0:
            x, xc = even_layer(x, xc, mod_lat, mod_ctx, norm_gain[l], even_w_in[i], q_a_gain[i],
                               kv_a_gain[i], w_uq[i], w_ukv[i], q_norm_gain[i], k_norm_gain[i],
                               lower_bounds[i], hg_norm_gain[i], even_w_out[i], (row, col), ctx_out)
        else:
            x, xc = odd_layer(x, xc, mod_lat, mod_ctx, norm_gain[l], odd_w_in[i], ret_decay[i],
                              odd_w_out[i], tpos, ctx_out)
    return x
```

```python
from contextlib import ExitStack
import numpy as np
import concourse.bass as bass
import concourse.mybir as mybir
from concourse.bass_utils import run_bass_kernel_spmd

F32 = mybir.dt.float32
BF16 = mybir.dt.bfloat16
U32 = mybir.dt.uint32
ALU = mybir.AluOpType
AF = mybir.ActivationFunctionType

ENGS = ["pe", "act", "dve", "pool", "sp"]
MULTIBLOCK = True
EPOCH = 30000
NDMASEM = 24

D = 4096
NCTX = 256
NLAT = 4096
NT = NCTX + NLAT
EPS = 1e-6
TOKBLKS = [(0, 256)] + [(256 + 512 * i, 512) for i in range(8)]
SB_A = [(0, 1280), (1280, 1024), (2304, 1024), (3328, 1024)]

E_CQ, E_CKV, E_KPE, E_GM, E_QH, E_FF, E_FB, E_IH, E_GH, E_END = 0, 1536, 2048, 2112, 4160, 6208, 8256, 10304, 12352, 14400
O_Q, O_K, O_V, O_G, O_END = 0, 4096, 8192, 16384, 24576
R_GM, R_QH, R_FF, R_FB, R_GH = E_GM + 64, E_QH + 64, E_FF + 64, E_FB + 64, E_GH + 64


class Buf:
    __slots__ = ("name", "w", "r")

    def __init__(self, name=""):
        self.name = name
        self.w = None
        self.r = {}


class Prog:
    def __init__(self, nc):
        self.nc = nc
        self.q = {e: [] for e in ENGS}
        self.cnt = {e: 0 for e in ENGS}
        self.known = {e: {} for e in ENGS}
        self.sems = {}
        self.semmax = {}
        self.dma_i = {e: 0 for e in ENGS}
        self.dma_cnt = {}
        self.eng_obj = {"pe": nc.tensor, "act": nc.scalar, "dve": nc.vector,
                        "pool": nc.gpsimd, "sp": nc.sync}

    def sem(self, key):
        s = self.sems.get(key)
        if s is None:
            s = self.nc.alloc_semaphore(name="s_%s" % "_".join(str(k) for k in key))
            self.sems[key] = s
        return s

    def _deps(self, eng, reads, writes, acc=()):
        deps = {}

        def add(k, v):
            if deps.get(k, 0) < v:
                deps[k] = v
        for b in reads:
            if b.w is not None:
                add(*b.w)
        for b in writes:
            if b.w is not None:
                if not (b in acc and b.w[0][0] == "e" and b.w[0][1] == eng):
                    add(*b.w)
            for k, v in b.r.items():
                add(k, v)
        kn = self.known[eng]
        waits = []
        for k, v in deps.items():
            if kn.get(k, 0) < v:
                kn[k] = v
                waits.append((k, v))
        return waits

    def _mark(self, key, v, reads, writes):
        for b in reads:
            if b.r.get(key, 0) < v:
                b.r[key] = v
        for b in writes:
            b.w = (key, v)
            b.r = {}
        self.semmax[key] = v

    def op(self, eng, fns, reads=(), writes=(), acc=()):
        if not isinstance(fns, (list, tuple)):
            fns = [fns]
        waits = self._deps(eng, reads, writes, acc)
        self.cnt[eng] += 1
        n = self.cnt[eng]
        ep, v = divmod(n - 1, EPOCH)
        key = ("e", eng, ep)
        self.sem(key)
        self.q[eng].append((waits, fns, key, 1))
        self._mark(key, v + 1, reads, writes)

    def dma(self, eng, out, in_, reads=(), writes=()):
        i = self.dma_i[eng]
        self.dma_i[eng] += 1
        key = ("d", eng, i % NDMASEM)
        self.sem(key)
        c = self.dma_cnt.get(key, 0)
        waits = self._deps(eng, reads, writes)
        if c > 0 and self.known[eng].get(key, 0) < 16 * c:
            self.known[eng][key] = 16 * c
            waits.append((key, 16 * c))
        self.dma_cnt[key] = c + 1
        e = self.eng_obj[eng]
        self.q[eng].append((waits, [lambda: e.dma_start(out=out, in_=in_)], key, 16))
        self._mark(key, 16 * (c + 1), reads, writes)

    def barrier(self):
        for eng in ENGS:
            kn = self.known[eng]
            waits = []
            for k, v in self.semmax.items():
                if kn.get(k, 0) < v:
                    kn[k] = v
                    waits.append((k, v))
            if waits:
                self.q[eng].append((waits, [], None, 0))

    def emit(self):
        nc = self.nc
        sems = self.sems

        def run(eng_name, engine):
            for waits, fns, key, inc in self.q[eng_name]:
                for k, v in waits:
                    engine.wait_ge(sems[k], v)
                for j, fn in enumerate(fns):
                    ins = fn()
                    if j == len(fns) - 1:
                        ins.then_inc(sems[key], inc)
        with nc.Block() as block:
            @block.tensor
            def _(e):
                run("pe", e)

            @block.scalar
            def _(e):
                run("act", e)

            @block.vector
            def _(e):
                run("dve", e)

            @block.gpsimd
            def _(e):
                run("pool", e)

            @block.sync
            def _(e):
                run("sp", e)
        self.q = {e: [] for e in ENGS}


class Tile:
    __slots__ = ("ap", "b")

    def __init__(self, ap, b=None):
        self.ap = ap
        self.b = b if b is not None else Buf()


class Ring:
    def __init__(self, tiles):
        self.t = tiles
        self.i = 0

    def next(self):
        t = self.t[self.i % len(self.t)]
        self.i += 1
        return t


class Builder:
    def __init__(self, layers=(0, 1, 2, 3), dbg=(), stop_after=None, debug_small=False, single=False):
        self.layers = list(layers)
        self.single = single
        self.small = debug_small or single
        if single:
            dbg = set(dbg) | {"XS"}
        self.dumps = []
        self.dbg = set(dbg)
        self.stop_after = stop_after
        nc = self.nc = bass.Bass("TRN2", target_bir_lowering=False)
        self.P = Prog(nc)
        self.stack = None
        self.tcount = 0
        self.pst = nc.alloc_psum_tensor("pst", [128, 1024], BF16)
        self.PS = [Tile(nc.alloc_psum_tensor("psf%d" % i, [128, 512], F32)[:, :]) for i in range(7)]
        self.PST = Tile(self.pst[:, :])
        self.PSTd = [Tile(self.pst[:, 0:512]), Tile(self.pst[:, 512:1024])]
        self.din = {}
        self.dram_bufs = {}
        self._declare()

    def inp(self, name, shape, dt=F32):
        self.din[name] = self.nc.dram_tensor(name, list(shape), dt, kind="ExternalInput").ap()
        return self.din[name]

    def scratch(self, name, shape, dt):
        kind = "ExternalOutput" if name in self.dbg else "Internal"
        t = self.nc.dram_tensor(name, list(shape), dt, kind=kind).ap()
        self.din[name] = t
        return t

    def yt(self, r0, nrows, t0, n):
        part = r0 // 6144
        assert (r0 + nrows - 1) // 6144 == part
        rr = r0 - part * 6144
        return self.din["YT%d" % part][rr:rr + nrows, t0:t0 + n]

    def dbuf(self, *key):
        b = self.dram_bufs.get(key)
        if b is None:
            b = self.dram_bufs[key] = Buf(str(key))
        return b

    def li(self, name, idx):
        return self.din[name][0 if self.small else idx]

    def _declare(self):
        def i(name, shape):
            if self.small and name in ("mod_w",):
                shape = [1, 128, 128]
            elif self.small and name in ("even_w_in", "w_uq", "w_ukv", "even_w_out", "odd_w_in", "odd_w_out"):
                need = (self.layers[0] % 2 == 0) == name.startswith(("even", "w_u"))
                shape = [1] + list(shape[1:]) if need else [1, 128, 128]
            return self.inp(name, shape)
        i("xT", [D, NT])
        i("cT", [128, 32, 2])
        i("mod_w", [4, D, 3 * D])
        i("mod_b", [4, 128, 96])
        i("norm_gain", [4, 128, 32])
        i("even_w_in", [2, D, E_END])
        i("q_a_gain", [2, 128, 12])
        i("kv_a_gain", [2, 128, 4])
        i("w_uq", [2, 1536, 3072])
        i("w_ukv", [2, 512, 4096])
        i("qk_gain", [2, 128, 4])
        i("hg_lb", [2, 128, 32])
        i("hg_norm_gain", [2, 128, 1])
        i("even_w_out", [2, D, D])
        i("odd_w_in", [2, D, O_END])
        i("ret_decay", [2, 128, 32])
        i("odd_w_out", [2, 2 * D, D])
        i("c_ident", [128, 128])
        i("c_ropeM", [2, 64, NT])
        i("c_rmat", [64, 64])
        i("c_ropeR", [2, 128, NT])
        i("c_scanmask", [128, NT])
        i("c_tri16", [2, 16, 16])
        i("c_tri128", [2, 128, 128])
        i("c_iota", [2, 128, 128])
        self.yT = self.nc.dram_tensor("yT", [D, NLAT], F32, kind="ExternalOutput").ap()
        s = self.scratch
        s("XS", [D, NT], F32)
        for part in range(4):
            s("YT%d" % part, [6144, NT], F32)
        s("VTOK", [NT, 8192], BF16)
        s("QT", [16, 192, NT], BF16)
        s("KT", [16, 192, NT], BF16)
        s("VA", [NT, 2048], BF16)
        s("OF", [8192, NT], F32)
        s("OB", [8192, NT], F32)
        s("MT", [8192, NT], BF16)
        if self.small:
            self.inp("MODV", [4, 128, 96, 2])
        else:
            s("MODV", [4, 128, 96, 2], F32)

    def reset(self):
        self.P.barrier()
        self.P.emit()
        if self.stack is not None:
            self.stack.close()
        self.stack = ExitStack()

    def al(self, shape, dt=F32):
        self.tcount += 1
        t = self.stack.enter_context(self.nc.sbuf_tensor("t%d" % self.tcount, list(shape), dt))
        return Tile(t[:])

    def ring(self, n, shape, dt=F32):
        return Ring([self.al(shape, dt) for _ in range(n)])

    def mm_fn(self, out, lhsT, rhs, start, stop):
        nc = self.nc
        return lambda: nc.tensor.matmul(out, lhsT=lhsT, rhs=rhs, start=start, stop=stop)

    def mm(self, out, lhsT, rhs, start=True, stop=True, reads=(), writes=(), acc=None):
        self.P.op("pe", self.mm_fn(out, lhsT, rhs, start, stop), reads, writes, acc=writes if acc is None else acc)

    def mmg(self, fns, reads, writes):
        self.P.op("pe", fns, reads, writes, acc=writes)

    def tr(self, out, in_, reads, writes):
        nc = self.nc
        ident = self.ident.ap
        self.P.op("pe", lambda: nc.tensor.transpose(out=out, in_=in_, identity=ident), list(reads) + [self.ident.b], writes, acc=writes)

    def act(self, out, in_, func, reads, writes, scale=1.0, bias=None):
        nc = self.nc
        if bias is None:
            self.P.op("act", lambda: nc.scalar.activation(out=out, in_=in_, func=func, scale=scale), reads, writes)
        else:
            self.P.op("act", lambda: nc.scalar.activation(out=out, in_=in_, func=func, scale=scale, bias=bias), reads, writes)

    def _v(self, eng):
        return self.nc.vector if eng == "dve" else self.nc.gpsimd

    def tt(self, eng, out, in0, in1, op, reads, writes):
        e = self._v(eng)
        self.P.op(eng, lambda: e.tensor_tensor(out=out, in0=in0, in1=in1, op=op), reads, writes)

    def ts(self, eng, out, in0, s1, op0, reads, writes, s2=None, op1=None):
        e = self._v(eng)
        if op1 is None:
            self.P.op(eng, lambda: e.tensor_scalar(out=out, in0=in0, scalar1=s1, scalar2=None, op0=op0), reads, writes)
        else:
            self.P.op(eng, lambda: e.tensor_scalar(out=out, in0=in0, scalar1=s1, scalar2=s2, op0=op0, op1=op1), reads, writes)

    def stt(self, eng, out, in0, scalar, in1, op0, op1, reads, writes):
        e = self._v(eng)
        self.P.op(eng, lambda: e.scalar_tensor_tensor(out=out, in0=in0, scalar=scalar, in1=in1, op0=op0, op1=op1), reads, writes)

    def cp(self, eng, out, in_, reads, writes):
        if eng == "act":
            nc = self.nc
            self.P.op("act", lambda: nc.scalar.copy(out=out, in_=in_), reads, writes)
        else:
            e = self._v(eng)
            self.P.op(eng, lambda: e.tensor_copy(out=out, in_=in_), reads, writes)

    def memset(self, eng, out, val, writes):
        e = self._v(eng)
        self.P.op(eng, lambda: e.memset(out, val), (), writes)

    def recip(self, out, in_, reads, writes):
        nc = self.nc
        self.P.op("dve", lambda: nc.vector.reciprocal(out=out, in_=in_), reads, writes)

    def rstd_from(self, dst, src_ap, src_bufs, inv_n, n):
        self.ts("dve", dst.ap[:, 0:n], src_ap, inv_n, ALU.mult, src_bufs, [dst.b], s2=EPS, op1=ALU.add)
        self.act(dst.ap[:, 0:n], dst.ap[:, 0:n], AF.Sqrt, [dst.b], [dst.b])
        self.recip(dst.ap[:, 0:n], dst.ap[:, 0:n], [dst.b], [dst.b])

    def load_consts(self):
        P = self.P
        self.ident_f = self.al([128, 128])
        self.ident = self.al([128, 128], BF16)
        self.ones = self.al([128, 128], BF16)
        P.dma("sp", self.ident_f.ap, self.din["c_ident"], writes=[self.ident_f.b])
        self.cp("dve", self.ident.ap, self.ident_f.ap, [self.ident_f.b], [self.ident.b])
        self.memset("dve", self.ones.ap, 1.0, [self.ones.b])

    def phase_mod(self, l):
        P, nc = self.P, self.nc
        self.reset()
        cT = self.al([128, 32, 2])
        sg = self.al([128, 32, 2])
        sc = self.al([128, 32, 2])
        mb = self.al([128, 96])
        mv = self.al([128, 96, 2])
        P.dma("sp", cT.ap, self.din["cT"], writes=[cT.b])
        P.dma("sp", mb.ap, self.din["mod_b"][l], writes=[mb.b])
        self.act(sg.ap, cT.ap, AF.Sigmoid, [cT.b], [sg.b])
        self.tt("dve", sc.ap, cT.ap, sg.ap, ALU.mult, [cT.b, sg.b], [sc.b])
        wr = self.ring(3, [128, 32, 256])
        ps = self.PS[0]
        psv = ps.ap[:, 0:192]
        wv = self.li("mod_w", l).rearrange("(k p) c -> p k c", p=128)
        qi = 0
        for jg in range(48):
            w = wr.next()
            for kq in range(4):
                P.dma("sp" if qi % 2 == 0 else "pool", w.ap[:, kq * 8:(kq + 1) * 8, :], wv[:, kq * 8:(kq + 1) * 8, jg * 256:(jg + 1) * 256], writes=[w.b])
                qi += 1
            for jj in range(2):
                j = jg * 2 + jj
                fns = [self.mm_fn(ps.ap[:, 2 * j:2 * j + 2], w.ap[:, k, jj * 128:(jj + 1) * 128], sc.ap[:, k, :], k == 0, k == 31) for k in range(32)]
                self.mmg(fns, [w.b, sc.b], [ps.b])
        self.tt("dve", mv.ap, psv.rearrange("p (a b) -> p a b", b=2), mb.ap.unsqueeze(2).to_broadcast([128, 96, 2]), ALU.add, [ps.b, mb.b], [mv.b])
        P.dma("sp", self.din["MODV"][l], mv.ap, reads=[mv.b], writes=[self.dbuf("MODV", l)])

    def load_mod(self, l):
        P = self.P
        mv = self.al([128, 96, 2])
        ng = self.al([128, 32])
        g1 = self.al([128, 32, 2])
        P.dma("sp", mv.ap, self.din["MODV"][l], reads=[self.dbuf("MODV", l)], writes=[mv.b])
        P.dma("sp", ng.ap, self.din["norm_gain"][l], writes=[ng.b])
        self.ts("dve", g1.ap, mv.ap[:, 32:64, :], 1.0, ALU.add, [mv.b], [g1.b])
        self.tt("dve", g1.ap, g1.ap, ng.ap.unsqueeze(2).to_broadcast([128, 32, 2]), ALU.mult, [g1.b, ng.b], [g1.b])
        return mv, g1

    def phase_inproj(self, l, xsrc, xsrc_key):
        P, nc = self.P, self.nc
        even = (l % 2 == 0)
        i = l // 2
        W = self.li("even_w_in" if even else "odd_w_in", i)
        Wv = W.rearrange("(k p) c -> p k c", p=128)
        VTOK = self.din["VTOK"]
        rshift = (lambda c: 64 if (even and c >= E_GM) else 0)
        if even:
            groups = []
            for c0 in range(E_CQ, E_KPE, 512):
                groups.append(("fm", c0, 512, False))
            groups.append(("fm", E_KPE, 64, False))
            for c0 in range(E_GM, E_QH, 512):
                groups.append(("fm", c0, 512, True))
            for c0 in range(E_QH, E_IH, 512):
                groups.append(("fm", c0, 512, False))
            for c0 in range(E_IH, E_GH, 512):
                groups.append(("tm", c0, 512, c0 - E_IH))
            for c0 in range(E_GH, E_END, 512):
                groups.append(("fm", c0, 512, True))
        else:
            groups = []
            for c0 in range(O_Q, O_V, 512):
                groups.append(("fm", c0, 512, False))
            for c0 in range(O_V, O_G, 512):
                groups.append(("tm", c0, 512, c0 - O_V))
            for c0 in range(O_G, O_END, 512):
                groups.append(("fm", c0, 512, True))
        self.reset()
        self.load_consts()
        mv, g1 = self.load_mod(l)
        hT = self.al([128, 32, 1280], BF16)
        wring = self.ring(2, [128, 32, 512], BF16)
        xin = self.ring(4, [128, 512])
        sq = self.ring(2, [128, 512], BF16)
        rstd = self.al([128, 512])
        tmp = self.ring(2, [128, 512])
        stage = self.ring(4, [128, 512])
        stage_b = self.ring(3, [128, 512], BF16)
        psr = Ring(self.PS[0:6])
        pstat = self.PS[6]
        ev = 0
        for (T0, TN) in SB_A:
            subs = [(s, n) for (s, n) in TOKBLKS if T0 <= s < T0 + TN]
            for (t0, n) in subs:
                m = 1 if t0 < NCTX else 0
                off = t0 - T0
                for k in range(32):
                    x = xin.next()
                    P.dma("sp", x.ap[:, 0:n], xsrc[k * 128:(k + 1) * 128, t0:t0 + n], reads=[self.dbuf(xsrc_key, k, t0)], writes=[x.b])
                    s_ = sq.next()
                    self.act(s_.ap[:, 0:n], x.ap[:, 0:n], AF.Square, [x.b], [s_.b])
                    self.mm(pstat.ap[:, 0:n], self.ones.ap, s_.ap[:, 0:n], k == 0, k == 31, [self.ones.b, s_.b], [pstat.b])
                self.rstd_from(rstd, pstat.ap[:, 0:n], [pstat.b], 1.0 / D, n)
                for k in range(32):
                    x = xin.next()
                    P.dma("sp", x.ap[:, 0:n], xsrc[k * 128:(k + 1) * 128, t0:t0 + n], reads=[self.dbuf(xsrc_key, k, t0)], writes=[x.b])
                    t_ = tmp.next()
                    self.tt("dve", t_.ap[:, 0:n], x.ap[:, 0:n], rstd.ap[:, 0:n], ALU.mult, [x.b, rstd.b], [t_.b])
                    self.act(hT.ap[:, k, off:off + n], t_.ap[:, 0:n], AF.Identity, [t_.b, g1.b, mv.b], [hT.b],
                             scale=g1.ap[:, k, m:m + 1], bias=mv.ap[:, k, m:m + 1])
            for (kind, c0, nc_, info) in groups:
                w = wring.next()
                for kq in range(4):
                    P.dma("pool", w.ap[:, kq * 8:(kq + 1) * 8, 0:nc_], Wv[:, kq * 8:(kq + 1) * 8, c0:c0 + nc_], writes=[w.b])
                if kind == "fm":
                    silu = info
                    for cc in range(0, nc_, 128):
                        rows = min(128, nc_ - cc)
                        for (t0, n) in subs:
                            off = t0 - T0
                            ps = psr.next()
                            fns = [self.mm_fn(ps.ap[0:rows, 0:n], w.ap[:, k, cc:cc + rows], hT.ap[:, k, off:off + n], k == 0, k == 31) for k in range(32)]
                            self.mmg(fns, [w.b, hT.b], [ps.b])
                            st = stage.next()
                            if silu:
                                self.act(st.ap[0:rows, 0:n], ps.ap[0:rows, 0:n], AF.Silu, [ps.b], [st.b])
                            else:
                                ev += 1
                                self.cp("dve" if ev % 3 else "act", st.ap[0:rows, 0:n], ps.ap[0:rows, 0:n], [ps.b], [st.b])
                            yr = c0 + cc + rshift(c0)
                            P.dma("sp", self.yt(yr, rows, t0, n), st.ap[0:rows, 0:n], reads=[st.b], writes=[self.dbuf("YT", yr, t0)])
                else:
                    vc0 = info
                    for tt_ in range(TN // 128):
                        ps = psr.next()
                        fns = [self.mm_fn(ps.ap[:, 0:512], hT.ap[:, k, tt_ * 128:(tt_ + 1) * 128], w.ap[:, k, 0:512], k == 0, k == 31) for k in range(32)]
                        self.mmg(fns, [w.b, hT.b], [ps.b])
                        st = stage_b.next()
                        ev += 1
                        self.cp("dve" if ev % 3 else "act", st.ap, ps.ap[:, 0:512], [ps.b], [st.b])
                        tg = T0 + tt_ * 128
                        P.dma("sp", VTOK[tg:tg + 128, vc0:vc0 + 512], st.ap, reads=[st.b], writes=[self.dbuf("VTOK", tg // 128, vc0)])

    def phase_mla_prep(self, i):
        P, nc = self.P, self.nc
        QT, KT, VA = self.din["QT"], self.din["KT"], self.din["VA"]
        self.reset()
        self.load_consts()
        wuq = self.al([128, 12, 3072], BF16)
        wukv = self.al([128, 4, 4096], BF16)
        Wq = self.li("w_uq", i).rearrange("(k p) c -> p k c", p=128)
        Wkv = self.li("w_ukv", i).rearrange("(k p) c -> p k c", p=128)
        for k in range(12):
            P.dma("pool", wuq.ap[:, k, :], Wq[:, k, :], writes=[wuq.b])
        for k in range(4):
            P.dma("pool", wukv.ap[:, k, :], Wkv[:, k, :], writes=[wukv.b])
        qag = self.al([128, 12])
        kvag = self.al([128, 4])
        qkg = self.al([128, 4])
        rmat = self.al([64, 64])
        P.dma("sp", qag.ap, self.din["q_a_gain"][i], writes=[qag.b])
        P.dma("sp", kvag.ap, self.din["kv_a_gain"][i], writes=[kvag.b])
        P.dma("sp", qkg.ap, self.din["qk_gain"][i], writes=[qkg.b])
        P.dma("sp", rmat.ap, self.din["c_rmat"], writes=[rmat.b])
        self.ts("dve", qkg.ap[:, 0:2], qkg.ap[:, 0:2], float(192.0 ** -0.5), ALU.mult, [qkg.b], [qkg.b])
        cq = self.al([128, 12, 512])
        cqn = self.al([128, 12, 512], BF16)
        ckv = self.al([128, 4, 512])
        ckvn = self.al([128, 4, 512], BF16)
        kpe = self.al([64, 512])
        kg = self.al([64, 512])
        kpr = self.al([64, 512])
        sqk = self.al([64, 512], BF16)
        cosT = self.al([64, 512])
        sinT = self.al([64, 512])
        sq = self.ring(3, [128, 512], BF16)
        rstd = self.al([128, 512])
        rh = self.ring(2, [128, 512])
        tmp = self.ring(4, [128, 512])
        stb = self.ring(4, [128, 512], BF16)
        psr = Ring(self.PS[0:4])
        psR = self.PS[4]
        psX = self.PS[5]
        pstat = self.PS[6]
        for (t0, n) in TOKBLKS:
            P.dma("sp", cosT.ap[:, 0:n], self.din["c_ropeM"][0, :, t0:t0 + n], writes=[cosT.b])
            P.dma("sp", sinT.ap[:, 0:n], self.din["c_ropeM"][1, :, t0:t0 + n], writes=[sinT.b])
            for src0, nk, raw, nrm, gain in ((E_CQ, 12, cq, cqn, qag), (E_CKV, 4, ckv, ckvn, kvag)):
                for k in range(nk):
                    r0 = src0 + k * 128
                    P.dma("sp", raw.ap[:, k, 0:n], self.yt(r0, 128, t0, n), reads=[self.dbuf("YT", r0, t0)], writes=[raw.b])
                for k in range(nk):
                    s_ = sq.next()
                    self.act(s_.ap[:, 0:n], raw.ap[:, k, 0:n], AF.Square, [raw.b], [s_.b])
                    self.mm(pstat.ap[:, 0:n], self.ones.ap, s_.ap[:, 0:n], k == 0, k == nk - 1, [self.ones.b, s_.b], [pstat.b])
                self.rstd_from(rstd, pstat.ap[:, 0:n], [pstat.b], 1.0 / (nk * 128), n)
                for k in range(nk):
                    t_ = tmp.next()
                    self.tt("dve", t_.ap[:, 0:n], raw.ap[:, k, 0:n], rstd.ap[:, 0:n], ALU.mult, [raw.b, rstd.b], [t_.b])
                    self.act(nrm.ap[:, k, 0:n], t_.ap[:, 0:n], AF.Identity, [t_.b, gain.b], [nrm.b], scale=gain.ap[:, k:k + 1])
            P.dma("sp", kpe.ap[:, 0:n], self.yt(E_KPE, 64, t0, n), reads=[self.dbuf("YT", E_KPE, t0)], writes=[kpe.b])
            self.act(sqk.ap[:, 0:n], kpe.ap[:, 0:n], AF.Square, [kpe.b], [sqk.b])
            self.act(kg.ap[:, 0:n], kpe.ap[:, 0:n], AF.Identity, [kpe.b, qkg.b], [kg.b], scale=qkg.ap[0:64, 3:4])
            self.mm(psR.ap[0:64, 0:n], rmat.ap, kg.ap[:, 0:n], True, True, [rmat.b, kg.b], [psR.b])
            t1 = tmp.next()
            self.tt("dve", t1.ap[0:64, 0:n], kg.ap[:, 0:n], cosT.ap[:, 0:n], ALU.mult, [kg.b, cosT.b], [t1.b])
            t2 = tmp.next()
            self.tt("dve", t2.ap[0:64, 0:n], psR.ap[0:64, 0:n], sinT.ap[:, 0:n], ALU.mult, [psR.b, sinT.b], [t2.b])
            self.tt("dve", kpr.ap[:, 0:n], t1.ap[0:64, 0:n], t2.ap[0:64, 0:n], ALU.add, [t1.b, t2.b], [kpr.b])
            for h in range(16):
                pa = psr.next()
                fns = [self.mm_fn(pa.ap[:, 0:n], wuq.ap[:, k, h * 192:h * 192 + 128], cqn.ap[:, k, 0:n], k == 0, k == 11) for k in range(12)]
                self.mmg(fns, [wuq.b, cqn.b], [pa.b])
                pb = psr.next()
                fns = [self.mm_fn(pb.ap[0:64, 0:n], wuq.ap[:, k, h * 192 + 128:h * 192 + 192], cqn.ap[:, k, 0:n], k == 0, k == 11) for k in range(12)]
                self.mmg(fns, [wuq.b, cqn.b], [pb.b])
                sa = sq.next()
                self.act(sa.ap[:, 0:n], pa.ap[:, 0:n], AF.Square, [pa.b], [sa.b])
                sb_ = sq.next()
                self.act(sb_.ap[0:64, 0:n], pb.ap[0:64, 0:n], AF.Square, [pb.b], [sb_.b])
                self.mm(pstat.ap[:, 0:n], self.ones.ap, sa.ap[:, 0:n], True, False, [self.ones.b, sa.b], [pstat.b])
                self.mm(pstat.ap[:, 0:n], self.ones.ap[0:64, :], sb_.ap[0:64, 0:n], False, True, [self.ones.b, sb_.b], [pstat.b])
                r_ = rh.next()
                self.rstd_from(r_, pstat.ap[:, 0:n], [pstat.b], 1.0 / 192, n)
                ta = tmp.next()
                self.tt("dve", ta.ap[:, 0:n], pa.ap[:, 0:n], r_.ap[:, 0:n], ALU.mult, [pa.b, r_.b], [ta.b])
                oa = stb.next()
                self.act(oa.ap[:, 0:n], ta.ap[:, 0:n], AF.Identity, [ta.b, qkg.b], [oa.b], scale=qkg.ap[:, 0:1])
                P.dma("sp", QT[h, 0:128, t0:t0 + n], oa.ap[:, 0:n], reads=[oa.b], writes=[self.dbuf("QT", h, t0)])
                tb = tmp.next()
                self.tt("dve", tb.ap[0:64, 0:n], pb.ap[0:64, 0:n], r_.ap[0:64, 0:n], ALU.mult, [pb.b, r_.b], [tb.b])
                tg = tmp.next()
                self.act(tg.ap[0:64, 0:n], tb.ap[0:64, 0:n], AF.Identity, [tb.b, qkg.b], [tg.b], scale=qkg.ap[0:64, 1:2])
                self.mm(psX.ap[0:64, 0:n], rmat.ap, tg.ap[0:64, 0:n], True, True, [rmat.b, tg.b], [psX.b])
                self.tt("dve", tb.ap[0:64, 0:n], tg.ap[0:64, 0:n], cosT.ap[:, 0:n], ALU.mult, [tg.b, cosT.b], [tb.b])
                t3 = tmp.next()
                self.tt("dve", t3.ap[0:64, 0:n], psX.ap[0:64, 0:n], sinT.ap[:, 0:n], ALU.mult, [psX.b, sinT.b], [t3.b])
                ob = stb.next()
                self.tt("dve", ob.ap[0:64, 0:n], tb.ap[0:64, 0:n], t3.ap[0:64, 0:n], ALU.add, [tb.b, t3.b], [ob.b])
                P.dma("sp", QT[h, 128:192, t0:t0 + n], ob.ap[0:64, 0:n], reads=[ob.b], writes=[self.dbuf("QT", h, t0)])
                pk = psr.next()
                fns = [self.mm_fn(pk.ap[:, 0:n], wukv.ap[:, k, h * 128:(h + 1) * 128], ckvn.ap[:, k, 0:n], k == 0, k == 3) for k in range(4)]
                self.mmg(fns, [wukv.b, ckvn.b], [pk.b])
                sk = sq.next()
                self.act(sk.ap[:, 0:n], pk.ap[:, 0:n], AF.Square, [pk.b], [sk.b])
                self.mm(pstat.ap[:, 0:n], self.ones.ap, sk.ap[:, 0:n], True, False, [self.ones.b, sk.b], [pstat.b])
                self.mm(pstat.ap[:, 0:n], self.ones.ap[0:64, :], sqk.ap[0:64, 0:n], False, True, [self.ones.b, sqk.b], [pstat.b])
                r2 = rh.next()
                self.rstd_from(r2, pstat.ap[:, 0:n], [pstat.b], 1.0 / 192, n)
                tk = tmp.next()
                self.tt("dve", tk.ap[:, 0:n], pk.ap[:, 0:n], r2.ap[:, 0:n], ALU.mult, [pk.b, r2.b], [tk.b])
                ok = stb.next()
                self.act(ok.ap[:, 0:n], tk.ap[:, 0:n], AF.Identity, [tk.b, qkg.b], [ok.b], scale=qkg.ap[:, 2:3])
                P.dma("sp", KT[h, 0:128, t0:t0 + n], ok.ap[:, 0:n], reads=[ok.b], writes=[self.dbuf("KT", h, t0)])
                okb = stb.next()
                self.tt("dve", okb.ap[0:64, 0:n], kpr.ap[:, 0:n], r2.ap[0:64, 0:n], ALU.mult, [kpr.b, r2.b], [okb.b])
                P.dma("sp", KT[h, 128:192, t0:t0 + n], okb.ap[0:64, 0:n], reads=[okb.b], writes=[self.dbuf("KT", h, t0)])
            for tt_ in range(n // 128):
                for vg in range(4):
                    ps = psr.next()
                    fns = [self.mm_fn(ps.ap[:, 0:512], ckvn.ap[:, k, tt_ * 128:(tt_ + 1) * 128], wukv.ap[:, k, 2048 + vg * 512:2048 + (vg + 1) * 512], k == 0, k == 3) for k in range(4)]
                    self.mmg(fns, [wukv.b, ckvn.b], [ps.b])
                    st = stb.next()
                    self.cp("act", st.ap, ps.ap[:, 0:512], [ps.b], [st.b])
                    tg_ = t0 + tt_ * 128
                    P.dma("sp", VA[tg_:tg_ + 128, vg * 512:(vg + 1) * 512], st.ap, reads=[st.b], writes=[self.dbuf("VA", tg_ // 128)])

    def phase_attn(self, i, with_ctx):
        P, nc = self.P, self.nc
        QT, KT, VA, MT = self.din["QT"], self.din["KT"], self.din["VA"], self.din["MT"]
        self.reset()
        self.load_consts()
        kAr = self.ring(2, [128, NT], BF16)
        kBr = self.ring(2, [64, NT], BF16)
        vr = self.ring(2, [128, 34, 128], BF16)
        qAr = self.ring(2, [128, 512], BF16)
        qBr = self.ring(2, [64, 512], BF16)
        ptr = self.ring(3, [128, 512], BF16)
        gmr = self.ring(2, [128, 512])
        rzr = self.ring(2, [128, 512])
        attr_ = self.ring(2, [128, 512])
        outr = self.ring(2, [128, 512], BF16)
        psS = Ring(self.PS[0:2])
        psO = Ring(self.PS[2:4])
        psZ = Ring(self.PS[4:6])
        allk = [self.dbuf("KT", h, t0) for h in range(16) for (t0, n) in TOKBLKS]
        for h in range(16):
            kA, kB, v = kAr.next(), kBr.next(), vr.next()
            kdeps = [self.dbuf("KT", h, t0) for (t0, n) in TOKBLKS]
            P.dma("sp", kA.ap, KT[h, 0:128, :], reads=kdeps, writes=[kA.b])
            P.dma("sp", kB.ap, KT[h, 128:192, :], reads=kdeps, writes=[kB.b])
            for half in range(2):
                P.dma("sp", v.ap[:, half * 17:(half + 1) * 17, :],
                      VA[half * 17 * 128:(half + 1) * 17 * 128, h * 128:(h + 1) * 128].rearrange("(t p) d -> p t d", p=128),
                      reads=[self.dbuf("VA", t) for t in range(half * 17, (half + 1) * 17)], writes=[v.b])
            for qi, (t0, n) in enumerate(TOKBLKS):
                if qi == 0 and not with_ctx:
                    continue
                keys = [0, 1] if qi == 0 else list(range(34))
                qA, qB = qAr.next(), qBr.next()
                P.dma("sp", qA.ap[:, 0:n], QT[h, 0:128, t0:t0 + n], reads=[self.dbuf("QT", h, t0)], writes=[qA.b])
                P.dma("sp", qB.ap[:, 0:n], QT[h, 128:192, t0:t0 + n], reads=[self.dbuf("QT", h, t0)], writes=[qB.b])
                gm = gmr.next()
                r0 = R_GM + h * 128
                P.dma("sp", gm.ap[:, 0:n], self.yt(r0, 128, t0, n), reads=[self.dbuf("YT", r0, t0)], writes=[gm.b])
                po, pz = psO.next(), psZ.next()

                def s_mm(kt):
                    ps = psS.next()
                    self.mmg([self.mm_fn(ps.ap[:, 0:n], kA.ap[:, kt * 128:(kt + 1) * 128], qA.ap[:, 0:n], True, False),
                              self.mm_fn(ps.ap[:, 0:n], kB.ap[:, kt * 128:(kt + 1) * 128], qB.ap[:, 0:n], False, True)],
                             [kA.b, kB.b, qA.b, qB.b], [ps.b])
                    return ps
                pend = s_mm(keys[0])
                for idx, kt in enumerate(keys):
                    ps = pend
                    pt = ptr.next()
                    self.act(pt.ap[:, 0:n], ps.ap[:, 0:n], AF.Exp, [ps.b], [pt.b])
                    if idx + 1 < len(keys):
                        pend = s_mm(keys[idx + 1])
                    first, last = idx == 0, idx == len(keys) - 1
                    self.mm(po.ap[:, 0:n], v.ap[:, kt, :], pt.ap[:, 0:n], first, last, [v.b, pt.b], [po.b])
                    self.mm(pz.ap[:, 0:n], self.ones.ap, pt.ap[:, 0:n], first, last, [self.ones.b, pt.b], [pz.b])
                rz = rzr.next()
                self.recip(rz.ap[:, 0:n], pz.ap[:, 0:n], [pz.b], [rz.b])
                at = attr_.next()
                self.tt("dve", at.ap[:, 0:n], po.ap[:, 0:n], rz.ap[:, 0:n], ALU.mult, [po.b, rz.b], [at.b])
                o = outr.next()
                self.tt("pool", o.ap[:, 0:n], at.ap[:, 0:n], gm.ap[:, 0:n], ALU.mult, [at.b, gm.b], [o.b])
                P.dma("sp", MT[h * 128:(h + 1) * 128, t0:t0 + n], o.ap[:, 0:n], reads=[o.b], writes=[self.dbuf("MT", h, t0)])

    def scan_chains(self, chains, C, ndk, ndv, vt, tri, row0):
        P, nc = self.P, self.nc
        nslot = 512 // (ndv * C)
        nsteps = len(chains[0]["order"])
        dv = ndv * 128

        def group_of(c):
            return Builder.group_of(c, C, ndv)
        for ch in chains:
            ch["cur"] = None
            ch["done"] = 0
        for step in range(nsteps):
            for ch in chains:
                c = ch["order"][step]
                d = ch["dir"]
                Qt, Kt, S, Sb = ch["Qt"], ch["Kt"], ch["S"], ch["Sb"]
                cs = slice(c * C, (c + 1) * C)
                gid, slot, tok0, ns = group_of(c)
                if ch["cur"] is None or ch["cur"][0] != gid:
                    ch["cur"] = (gid, ch["psO"].next(), tok0, ns)
                    ch["done"] = 0
                po = ch["cur"][1]
                pa = ch["psA"].next()
                self.mmg([self.mm_fn(pa.ap[0:C, 0:C], Kt.ap[:, dc, cs], Qt.ap[:, dc, cs], dc == 0, dc == ndk - 1) for dc in range(ndk)],
                         [Kt.b, Qt.b], [pa.b])
                am = ch["am"].next()
                self.tt("dve", am.ap[0:C, 0:C], pa.ap[0:C, 0:C], tri.ap[0:C, d, :], ALU.mult, [pa.b, tri.b], [am.b])
                kd = ch["kd"].next()
                for dc in range(ndk):
                    kdt = ch["kdt"].next()
                    self.act(kdt.ap[:, 0:C], Kt.ap[:, dc, cs], AF.Identity, [Kt.b, ch["ELb"]], [kdt.b], scale=ch["EL"](dc, c))
                    self.tr(self.PSTd[d].ap[0:C, dc * 128:(dc + 1) * 128], kdt.ap[:, 0:C], [kdt.b], [self.PSTd[d].b])
                self.cp("dve", kd.ap[0:C, 0:ndk * 128], self.PSTd[d].ap[0:C, 0:ndk * 128], [self.PSTd[d].b], [kd.b])
                fns = []
                for dvc in range(ndv):
                    col = slot * (ndv * C) + dvc * C
                    fns.append(self.mm_fn(po.ap[:, col:col + C], vt.ap[0:C, c, dvc * 128:(dvc + 1) * 128], am.ap[0:C, 0:C], True, False))
                    for dc in range(ndk):
                        fns.append(self.mm_fn(po.ap[:, col:col + C], Sb.ap[:, dc, dvc * 128:(dvc + 1) * 128], Qt.ap[:, dc, cs], False, dc == ndk - 1))
                self.mmg(fns, [vt.b, am.b, Sb.b, Qt.b], [po.b])
                ch["done"] += 1
                if ch["done"] == ns:
                    st = ch["ost"].next()
                    w = ns * ndv * C
                    self.cp("act", st.ap[:, 0:w], po.ap[:, 0:w], [po.b], [st.b])
                    for dvc in range(ndv):
                        r0 = row0 + dvc * 128
                        src = st.ap[:, 0:w].rearrange("p (s v t) -> p s v t", s=ns, v=ndv)[:, :, dvc, :]
                        dstap = ch["dst"][r0:r0 + 128, tok0:tok0 + ns * C].rearrange("p (s t) -> p s t", s=ns)
                        P.dma("sp", dstap, src, reads=[st.b], writes=[self.dbuf(ch["dstkey"], r0, tok0)])
                for dc in range(ndk):
                    pS = ch["psS"][dc]
                    self.mm(pS.ap[:, 0:dv], kd.ap[0:C, dc * 128:(dc + 1) * 128], vt.ap[0:C, c, :], True, True, [kd.b, vt.b], [pS.b])
                    self.stt("dve", S.ap[:, dc, :], S.ap[:, dc, :], ch["EL"](dc, c), pS.ap[:, 0:dv], ALU.mult, ALU.add, [S.b, pS.b, ch["ELb"]], [S.b])
                self.cp("act", Sb.ap, S.ap, [S.b], [Sb.b])

    @staticmethod
    def group_of(c, C, ndv):
        nslot = 512 // (ndv * C)
        ctxc = NCTX // C
        if c < ctxc:
            gs = min(nslot, ctxc)
            return (c // gs, c % gs, (c // gs) * gs * C, gs)
        g = (c - ctxc) // nslot
        return (1000 + g, (c - ctxc) % nslot, NCTX + g * nslot * C, nslot)

    @staticmethod
    def orders(C):
        nch = NT // C
        nctx = NCTX // C
        fwd = list(range(nch))
        bwd = list(range(nctx - 1, -1, -1)) + list(range(nch - 1, nctx - 1, -1))
        return fwd, bwd

    def phase_hgrn(self, i):
        P, nc = self.P, self.nc
        VTOK, OF, OB = self.din["VTOK"], self.din["OF"], self.din["OB"]
        C = 16
        NCH = NT // C
        self.reset()
        self.load_consts()
        tri = self.al([C, 2, C])
        P.dma("sp", tri.ap, self.din["c_tri16"].rearrange("d s t -> s d t"), writes=[tri.b])
        smask = self.al([128, NT])
        P.dma("sp", smask.ap, self.din["c_scanmask"], writes=[smask.b])
        lb = self.al([128, 32])
        omlb = self.al([128, 32])
        if i == 0:
            self.memset("dve", lb.ap, 0.0, [lb.b])
        else:
            l0 = self.al([128, 32])
            l1 = self.al([128, 32])
            P.dma("sp", l0.ap, self.din["hg_lb"][0], writes=[l0.b])
            P.dma("sp", l1.ap, self.din["hg_lb"][1], writes=[l1.b])
            self.tt("dve", l1.ap, l1.ap, l0.ap, ALU.subtract, [l1.b, l0.b], [l1.b])
            self.act(lb.ap, l1.ap, AF.Sigmoid, [l1.b], [lb.b])
        self.ts("dve", omlb.ap, lb.ap, -1.0, ALU.mult, [lb.b], [omlb.b], s2=1.0, op1=ALU.add)
        a1 = self.al([128, NT])
        a2 = self.al([128, NT])
        a3 = self.al([128, NT])
        QK = {d: (self.al([128, 1, NT], BF16), self.al([128, 1, NT], BF16)) for d in (0, 1)}
        EL = {d: self.al([128, NCH]) for d in (0, 1)}
        Tt = self.al([128, NCH])
        vt = self.al([C, NCH, 128], BF16)
        S = {d: self.al([128, 1, 128]) for d in (0, 1)}
        Sb = {d: self.al([128, 1, 128], BF16) for d in (0, 1)}
        aux = {d: dict(am=self.ring(2, [C, C], BF16), kd=self.ring(2, [C, 128], BF16), kdt=self.ring(2, [128, C], BF16),
                       ost=self.ring(2, [128, 512])) for d in (0, 1)}
        psA_tiles = [Tile(self.PS[4].ap[:, s * 32:(s + 1) * 32]) for s in range(8)]
        fwd, bwd = self.orders(C)
        e_ = nc.vector
        for h in range(16):
            rq = R_QH + h * 128
            P.dma("sp", vt.ap, VTOK[:, h * 128:(h + 1) * 128].rearrange("(c s) v -> s c v", s=C),
                  reads=[self.dbuf("VTOK", t, vc) for t in range(34) for vc in (0, 512, 1024, 1536)], writes=[vt.b])
            for d in (0, 1):
                rf = (R_FF if d == 0 else R_FB) + h * 128
                col = d * 16 + h
                Qt, Kt = QK[d]
                P.dma("sp", a1.ap, self.yt(rf, 128, 0, NT), reads=[self.dbuf("YT", rf, t0) for (t0, n) in TOKBLKS], writes=[a1.b])
                self.act(a1.ap, a1.ap, AF.Sigmoid, [a1.b], [a1.b])
                self.ts("dve", a1.ap, a1.ap, omlb.ap[:, col:col + 1], ALU.mult, [a1.b, omlb.b, lb.b], [a1.b], s2=lb.ap[:, col:col + 1], op1=ALU.add)
                self.ts("pool", a3.ap, a1.ap, -1.0, ALU.mult, [a1.b], [a3.b], s2=1.0, op1=ALU.add)
                self.act(a2.ap, a1.ap, AF.Ln, [a1.b], [a2.b])
                self.P.op("dve", (lambda o=a1.ap, m=smask.ap, x=a2.ap: e_.tensor_tensor_scan(out=o, data0=m, data1=x, initial=0.0, op0=ALU.mult, op1=ALU.add)),
                          [smask.b, a2.b], [a1.b])
                bi3 = a1.ap.rearrange("p (c s) -> p c s", s=C)
                self.cp("pool", Tt.ap, bi3[:, :, C - 1], [a1.b], [Tt.b])
                if d == 0:
                    b_, other = a1, a2
                else:
                    self.tt("dve", a2.ap, a2.ap, a1.ap, ALU.subtract, [a2.b, a1.b], [a2.b])
                    self.tt("dve", a2.ap.rearrange("p (c s) -> p c s", s=C), a2.ap.rearrange("p (c s) -> p c s", s=C),
                            Tt.ap.unsqueeze(2).to_broadcast([128, NCH, C]), ALU.add, [a2.b, Tt.b], [a2.b])
                    b_, other = a2, a1
                self.act(EL[d].ap, Tt.ap, AF.Exp, [Tt.b], [EL[d].b])
                self.act(other.ap, b_.ap, AF.Exp, [b_.b], [other.b], scale=-1.0)
                self.tt("pool", Kt.ap[:, 0, :], a3.ap, other.ap, ALU.mult, [a3.b, other.b], [Kt.b])
                self.act(other.ap, b_.ap, AF.Exp, [b_.b], [other.b])
                P.dma("sp", a3.ap, self.yt(rq, 128, 0, NT), reads=[self.dbuf("YT", rq, t0) for (t0, n) in TOKBLKS], writes=[a3.b])
                self.tt("dve", Qt.ap[:, 0, :], a3.ap, other.ap, ALU.mult, [a3.b, other.b], [Qt.b])
                self.memset("dve", S[d].ap, 0.0, [S[d].b])
                self.memset("dve", Sb[d].ap, 0.0, [Sb[d].b])
            chains = []
            for d in (0, 1):
                el = EL[d]
                chains.append(dict(Qt=QK[d][0], Kt=QK[d][1], EL=(lambda dc, c, el=el: el.ap[:, c:c + 1]), ELb=el.b,
                                   order=fwd if d == 0 else bwd, S=S[d], Sb=Sb[d], dir=d,
                                   psO=Ring(self.PS[0:2] if d == 0 else self.PS[2:4]),
                                   psA=Ring(psA_tiles[d * 4:(d + 1) * 4]), psS=[self.PS[5 + d]],
                                   dst=OF if d == 0 else OB, dstkey="OF" if d == 0 else "OB", **aux[d]))
            self.scan_chains(chains, C, 1, 1, vt, tri, h * 128)

    def phase_ret(self, i):
        P, nc = self.P, self.nc
        VTOK, OF, OB = self.din["VTOK"], self.din["OF"], self.din["OB"]
        C = 128
        NCH = NT // C
        self.reset()
        self.load_consts()
        tri = self.al([128, 2, 128])
        P.dma("sp", tri.ap, self.din["c_tri128"].rearrange("d s t -> s d t"), writes=[tri.b])
        iota = self.al([128, 2, 128])
        P.dma("sp", iota.ap, self.din["c_iota"].rearrange("d p j -> p d j"), writes=[iota.b])
        rd = self.al([128, 32])
        y = self.al([128, 32])
        lg = self.al([128, 32])
        nlg = self.al([128, 32])
        P.dma("sp", rd.ap, self.din["ret_decay"][i], writes=[rd.b])
        self.act(y.ap, rd.ap, AF.Exp, [rd.b], [y.b], scale=-1.0)
        NTERM = 10
        self.memset("dve", lg.ap, float(((-1) ** (NTERM + 1)) / NTERM), [lg.b])
        for kk in range(NTERM - 1, 0, -1):
            self.tt("dve", lg.ap, lg.ap, y.ap, ALU.mult, [lg.b, y.b], [lg.b])
            self.ts("dve", lg.ap, lg.ap, float(((-1) ** (kk + 1)) / kk), ALU.add, [lg.b], [lg.b])
        self.tt("dve", nlg.ap, lg.ap, y.ap, ALU.mult, [lg.b, y.b], [nlg.b])
        self.ts("dve", lg.ap, nlg.ap, -1.0, ALU.mult, [nlg.b], [lg.b])
        el = self.al([128, 32])
        self.act(el.ap, lg.ap, AF.Exp, [lg.b], [el.b], scale=float(C))
        tabs = self.al([128, 4, 128])
        QK = {d: (self.al([128, 2, NT], BF16), self.al([128, 2, NT], BF16)) for d in (0, 1)}
        vt = self.al([128, NCH, 512], BF16)
        S = {d: self.al([128, 2, 512]) for d in (0, 1)}
        Sb = {d: self.al([128, 2, 512], BF16) for d in (0, 1)}
        aux = {d: dict(am=self.ring(2, [128, 128], BF16), kd=self.ring(2, [128, 256], BF16), kdt=self.ring(2, [128, 128], BF16),
                       ost=self.ring(2, [128, 512])) for d in (0, 1)}
        x1r = self.ring(2, [128, 512])
        x2r = self.ring(2, [128, 512])
        cosr = self.ring(2, [128, 512])
        sinr = self.ring(2, [128, 512])
        tr_ = self.ring(6, [128, 512])
        fwd, bwd = self.orders(C)
        ropeR = self.din["c_ropeR"]
        for h in range(16):
            P.dma("sp", vt.ap, VTOK[:, h * 512:(h + 1) * 512].rearrange("(c s) v -> s c v", s=C),
                  reads=[self.dbuf("VTOK", t, h * 512) for t in range(34)], writes=[vt.b])
            for d in (0, 1):
                col = d * 16 + h
                self.act(tabs.ap[:, 2 * d, :], iota.ap[:, d, :], AF.Exp, [iota.b, lg.b], [tabs.b], scale=lg.ap[:, col:col + 1])
                self.act(tabs.ap[:, 2 * d + 1, :], iota.ap[:, d, :], AF.Exp, [iota.b, nlg.b], [tabs.b], scale=nlg.ap[:, col:col + 1])
                self.ts("dve", tabs.ap[:, 2 * d + 1, :], tabs.ap[:, 2 * d + 1, :], 1.0 / 16.0, ALU.mult, [tabs.b], [tabs.b])
            for which in (0, 1):
                base = (O_Q if which == 0 else O_K) + h * 256
                for (t0, n) in TOKBLKS:
                    x1, x2, cs_, sn_ = x1r.next(), x2r.next(), cosr.next(), sinr.next()
                    P.dma("sp", x1.ap[:, 0:n], self.yt(base, 128, t0, n), reads=[self.dbuf("YT", base, t0)], writes=[x1.b])
                    P.dma("sp", x2.ap[:, 0:n], self.yt(base + 128, 128, t0, n), reads=[self.dbuf("YT", base + 128, t0)], writes=[x2.b])
                    P.dma("sp", cs_.ap[:, 0:n], ropeR[0, :, t0:t0 + n], writes=[cs_.b])
                    P.dma("sp", sn_.ap[:, 0:n], ropeR[1, :, t0:t0 + n], writes=[sn_.b])
                    t1, t2, t3, t4 = tr_.next(), tr_.next(), tr_.next(), tr_.next()
                    self.tt("dve", t1.ap[:, 0:n], x1.ap[:, 0:n], cs_.ap[:, 0:n], ALU.mult, [x1.b, cs_.b], [t1.b])
                    self.tt("pool", t2.ap[:, 0:n], x2.ap[:, 0:n], sn_.ap[:, 0:n], ALU.mult, [x2.b, sn_.b], [t2.b])
                    self.tt("dve", t1.ap[:, 0:n], t1.ap[:, 0:n], t2.ap[:, 0:n], ALU.subtract, [t1.b, t2.b], [t1.b])
                    self.tt("pool", t3.ap[:, 0:n], x1.ap[:, 0:n], sn_.ap[:, 0:n], ALU.mult, [x1.b, sn_.b], [t3.b])
                    self.tt("dve", t4.ap[:, 0:n], x2.ap[:, 0:n], cs_.ap[:, 0:n], ALU.mult, [x2.b, cs_.b], [t4.b])
                    self.tt("pool", t3.ap[:, 0:n], t3.ap[:, 0:n], t4.ap[:, 0:n], ALU.add, [t3.b, t4.b], [t3.b])
                    nck = n // C
                    for d in (0, 1):
                        dstT = QK[d][which]
                        tab = tabs.ap[:, 2 * d + which, :].unsqueeze(1).to_broadcast([128, nck, C])
                        for dc, src in ((0, t1), (1, t3)):
                            self.tt("dve" if dc == 0 else "pool",
                                    dstT.ap[:, dc, t0:t0 + n].rearrange("p (c s) -> p c s", s=C),
                                    src.ap[:, 0:n].rearrange("p (c s) -> p c s", s=C), tab, ALU.mult, [src.b, tabs.b], [dstT.b])
            for d in (0, 1):
                self.memset("dve", S[d].ap, 0.0, [S[d].b])
                self.memset("dve", Sb[d].ap, 0.0, [Sb[d].b])
            chains = []
            for d in (0, 1):
                col = d * 16 + h
                chains.append(dict(Qt=QK[d][0], Kt=QK[d][1], EL=(lambda dc, c, col=col: el.ap[:, col:col + 1]), ELb=el.b,
                                   order=fwd if d == 0 else bwd, S=S[d], Sb=Sb[d], dir=d,
                                   psO=Ring([self.PS[d]]), psA=Ring([self.PS[2 + d]]), psS=[self.PS[4], self.PS[5]],
                                   dst=OF if d == 0 else OB, dstkey="OF" if d == 0 else "OB", **aux[d]))
            self.scan_chains(chains, C, 2, 4, vt, tri, h * 512)

    def phase_post(self, l):
        P, nc = self.P, self.nc
        OF, OB, MT = self.din["OF"], self.din["OB"], self.din["MT"]
        even = (l % 2 == 0)
        ndv = 1 if even else 4
        self.reset()
        self.load_consts()
        hgg = self.al([128, 1])
        if even:
            P.dma("sp", hgg.ap, self.din["hg_norm_gain"][l // 2], writes=[hgg.b])
        else:
            self.memset("dve", hgg.ap, 1.0, [hgg.b])
        ofr = self.ring(2, [128, 4, 512])
        obr = self.ring(2, [128, 4, 512])
        gr = self.ring(2, [128, 4, 512])
        sq = self.ring(2, [128, 512], BF16)
        rstd = self.ring(2, [128, 512])
        outr = self.ring(3, [128, 512], BF16)
        pstat = Ring(self.PS[0:2])
        Cc = 16 if even else 128
        for h in range(16):
            for (t0, n) in TOKBLKS:
                if l == 3 and t0 < NCTX:
                    continue
                of, ob, g = ofr.next(), obr.next(), gr.next()
                for dvc in range(ndv):
                    r0 = h * ndv * 128 + dvc * 128
                    gr0 = (R_GH if even else O_G) + r0
                    okeys = set()
                    for tt_ in range(t0, t0 + n, Cc):
                        okeys.add(Builder.group_of(tt_ // Cc, Cc, ndv)[2])
                    P.dma("sp", of.ap[:, dvc, 0:n], OF[r0:r0 + 128, t0:t0 + n], reads=[self.dbuf("OF", r0, k_) for k_ in okeys], writes=[of.b])
                    P.dma("sp", ob.ap[:, dvc, 0:n], OB[r0:r0 + 128, t0:t0 + n], reads=[self.dbuf("OB", r0, k_) for k_ in okeys], writes=[ob.b])
                    P.dma("sp", g.ap[:, dvc, 0:n], self.yt(gr0, 128, t0, n), reads=[self.dbuf("YT", gr0, t0)], writes=[g.b])
                self.tt("pool", of.ap[:, 0:ndv, 0:n], of.ap[:, 0:ndv, 0:n], ob.ap[:, 0:ndv, 0:n], ALU.add, [of.b, ob.b], [of.b])
                ps = pstat.next()
                for dvc in range(ndv):
                    s_ = sq.next()
                    self.act(s_.ap[:, 0:n], of.ap[:, dvc, 0:n], AF.Square, [of.b], [s_.b])
                    self.mm(ps.ap[:, 0:n], self.ones.ap, s_.ap[:, 0:n], dvc == 0, dvc == ndv - 1, [self.ones.b, s_.b], [ps.b])
                r_ = rstd.next()
                self.rstd_from(r_, ps.ap[:, 0:n], [ps.b], 1.0 / (ndv * 128), n)
                for dvc in range(ndv):
                    r0 = h * ndv * 128 + dvc * 128
                    self.tt("dve", of.ap[:, dvc, 0:n], of.ap[:, dvc, 0:n], r_.ap[:, 0:n], ALU.mult, [of.b, r_.b], [of.b])
                    o = outr.next()
                    self.stt("dve", o.ap[:, 0:n], of.ap[:, dvc, 0:n], hgg.ap[:, 0:1], g.ap[:, dvc, 0:n], ALU.mult, ALU.mult, [of.b, hgg.b, g.b], [o.b])
                    mr0 = (2048 if even else 0) + r0
                    P.dma("sp", MT[mr0:mr0 + 128, t0:t0 + n], o.ap[:, 0:n], reads=[o.b], writes=[self.dbuf("MT", mr0 // 128, t0)])

    def phase_outproj(self, l, xsrc, xsrc_key, xdst, xdst_key, lat_only):
        P, nc = self.P, self.nc
        even = (l % 2 == 0)
        i = l // 2
        MT = self.din["MT"]
        W = self.li("even_w_out" if even else "odd_w_out", i)
        Wv = W.rearrange("(k p) c -> p k c", p=128)
        nk = 32 if even else 64
        GC = 512 if even else 256
        self.reset()
        self.load_consts()
        mv, g1 = self.load_mod(l)
        if even:
            SBs = SB_A
            mT = self.al([128, 32, 1280], BF16)
        else:
            SBs = TOKBLKS
            mT = self.al([128, 64, 512], BF16)
        wring = self.ring(2, [128, nk, GC], BF16)
        xr = self.ring(3, [128, 512])
        outr = self.ring(3, [128, 512])
        psr = Ring(self.PS[0:6])
        for (T0, TN) in SBs:
            subs = [(s, n) for (s, n) in TOKBLKS if T0 <= s < T0 + TN]
            if lat_only:
                subs = [(s, n) for (s, n) in subs if s >= NCTX]
            if not subs:
                continue
            for (t0, n) in subs:
                off = t0 - T0
                for kq in range(0, nk, 8):
                    P.dma("sp", mT.ap[:, kq:kq + 8, off:off + n],
                          MT[kq * 128:(kq + 8) * 128, t0:t0 + n].rearrange("(k p) t -> p k t", p=128),
                          reads=[self.dbuf("MT", kk, t0) for kk in range(kq, kq + 8)], writes=[mT.b])
            for c0 in range(0, D, GC):
                w = wring.next()
                for kq in range(0, nk, 8):
                    P.dma("pool", w.ap[:, kq:kq + 8, :], Wv[:, kq:kq + 8, c0:c0 + GC], writes=[w.b])
                for cc in range(0, GC, 128):
                    j = (c0 + cc) // 128
                    for (t0, n) in subs:
                        off = t0 - T0
                        m = 1 if t0 < NCTX else 0
                        ps = psr.next()
                        fns = [self.mm_fn(ps.ap[:, 0:n], w.ap[:, k, cc:cc + 128], mT.ap[:, k, off:off + n], k == 0, k == nk - 1) for k in range(nk)]
                        self.mmg(fns, [w.b, mT.b], [ps.b])
                        x = xr.next()
                        P.dma("sp", x.ap[:, 0:n], xsrc[j * 128:(j + 1) * 128, t0:t0 + n], reads=[self.dbuf(xsrc_key, j, t0)], writes=[x.b])
                        o = outr.next()
                        self.stt("dve", o.ap[:, 0:n], ps.ap[:, 0:n], mv.ap[:, 64 + j, m:m + 1], x.ap[:, 0:n], ALU.mult, ALU.add, [ps.b, mv.b, x.b], [o.b])
                        if xdst_key == "yT":
                            dst = xdst[j * 128:(j + 1) * 128, t0 - NCTX:t0 - NCTX + n]
                        else:
                            dst = xdst[j * 128:(j + 1) * 128, t0:t0 + n]
                        P.dma("sp", dst, o.ap[:, 0:n], reads=[o.b], writes=[self.dbuf(xdst_key, j, t0)])

    def build(self):
        XS = self.din["XS"]
        xT = self.din["xT"]
        first = True
        if not self.small:
            for l in self.layers:
                self.phase_mod(l)
        for l in self.layers:
            src, skey = (xT, "xT") if first else (XS, "XS")
            first = False
            self.phase_inproj(l, src, skey)
            if self.stop_after == ("inproj", l):
                break
            if l % 2 == 0:
                self.phase_mla_prep(l // 2)
                if self.stop_after == ("mla_prep", l):
                    break
                self.phase_attn(l // 2, with_ctx=(l < 3))
                if self.stop_after == ("attn", l):
                    break
                self.phase_hgrn(l // 2)
                if self.stop_after == ("scan", l):
                    break
            else:
                self.phase_ret(l // 2)
                if self.stop_after == ("scan", l):
                    break
            self.phase_post(l)
            if self.stop_after == ("post", l):
                break
            last = (l == self.layers[-1])
            if last and l == 3:
                self.phase_outproj(l, src, skey, self.yT, "yT", True)
            else:
                self.phase_outproj(l, src, skey, XS, "XS", False)
        self.P.barrier()
        for di, (name, rs, cs) in enumerate(self.dumps):
            if name == "YT":
                src = self.yt(rs[0], rs[1] - rs[0], cs[0], cs[1] - cs[0])
            elif ":" in name:
                nm, hh = name.split(":")
                src = self.din[nm][int(hh), rs[0]:rs[1], cs[0]:cs[1]]
            else:
                src = self.din[name][rs[0]:rs[1], cs[0]:cs[1]]
            o = self.nc.dram_tensor("dump%d" % di, [rs[1] - rs[0], cs[1] - cs[0]], src.dtype, kind="ExternalOutput").ap()
            self.P.dma("sp", o, src)
        self.P.barrier()
        self.P.emit()
        if self.stack is not None:
            self.stack.close()
        return self.nc


def build_mod_program():
    nc = bass.Bass("TRN2", target_bir_lowering=False)
    P = Prog(nc)
    cT_d = nc.dram_tensor("cT5", [128, 32, 5], F32, kind="ExternalInput").ap()
    w_d = nc.dram_tensor("mod_w1", [D, 3 * D], F32, kind="ExternalInput").ap()
    b_d = nc.dram_tensor("mod_b1", [128, 96], F32, kind="ExternalInput").ap()
    o_d = nc.dram_tensor("MODO", [128, 96, 5], F32, kind="ExternalOutput").ap()
    ps = nc.alloc_psum_tensor("psm", [128, 512], F32)
    T = lambda name, shape: Tile(nc.alloc_sbuf_tensor(name, shape, F32)[:])
    cT, sg, sc, mb, mv = T("cT", [128, 32, 5]), T("sg", [128, 32, 5]), T("sc", [128, 32, 5]), T("mb", [128, 96]), T("mv", [128, 96, 5])
    wr = Ring([Tile(nc.alloc_sbuf_tensor("w%d" % j, [128, 32, 256], F32)[:]) for j in range(3)])
    pt = Tile(ps[:, :])
    P.dma("sp", cT.ap, cT_d, writes=[cT.b])
    P.dma("sp", mb.ap, b_d, writes=[mb.b])
    P.op("act", lambda: nc.scalar.activation(out=sg.ap, in_=cT.ap, func=AF.Sigmoid), [cT.b], [sg.b])
    P.op("dve", lambda: nc.vector.tensor_tensor(out=sc.ap, in0=cT.ap, in1=sg.ap, op=ALU.mult), [cT.b, sg.b], [sc.b])

    def mmf(out, lhsT, rhs, start, stop):
        return lambda: nc.tensor.matmul(out, lhsT=lhsT, rhs=rhs, start=start, stop=stop)
    wv = w_d.rearrange("(k p) c -> p k c", p=128)
    qi = 0
    for jg in range(48):
        w = wr.next()
        for kq in range(4):
            P.dma("sp" if qi % 2 == 0 else "pool", w.ap[:, kq * 8:(kq + 1) * 8, :], wv[:, kq * 8:(kq + 1) * 8, jg * 256:(jg + 1) * 256], writes=[w.b])
            qi += 1
        for jj in range(2):
            j = jg * 2 + jj
            fns = [mmf(pt.ap[:, 5 * j:5 * j + 5], w.ap[:, k, jj * 128:(jj + 1) * 128], sc.ap[:, k, :], k == 0, k == 31) for k in range(32)]
            P.op("pe", fns, [w.b, sc.b], [pt.b], acc=[pt.b])
    P.op("dve", lambda: nc.vector.tensor_tensor(out=mv.ap, in0=pt.ap[:, 0:480].rearrange("p (a b) -> p a b", b=5),
                                                in1=mb.ap.unsqueeze(2).to_broadcast([128, 96, 5]), op=ALU.add), [pt.b, mb.b], [mv.b])
    ob = Buf()
    P.dma("sp", o_d, mv.ap, reads=[mv.b], writes=[ob])
    P.barrier()
    P.emit()
    return nc


def _chunks(v, nchunk):
    return np.ascontiguousarray(np.asarray(v, np.float32).reshape(nchunk, 128).T)


def _consts():
    f32 = np.float32
    c = {}
    c["c_ident"] = np.eye(128, dtype=f32)
    rows = NLAT // 64
    row = np.repeat(np.arange(rows), 64).astype(f32)
    col = np.tile(np.arange(64), rows).astype(f32)
    half = 32
    inv = (f32(10000.0) ** (-np.arange(0, half, 2, dtype=f32) / f32(half))).astype(f32)
    cosT = np.ones((64, NT), f32)
    sinT = np.zeros((64, NT), f32)
    for g, pos in enumerate((row, col)):
        ang = (pos[:, None] * inv[None, :]).astype(f32)
        cs, sn = np.cos(ang).astype(f32).T, np.sin(ang).astype(f32).T
        cosT[g * 32:g * 32 + 16, NCTX:] = cs
        cosT[g * 32 + 16:g * 32 + 32, NCTX:] = cs
        sinT[g * 32:g * 32 + 16, NCTX:] = -sn
        sinT[g * 32 + 16:g * 32 + 32, NCTX:] = sn
    c["c_ropeM"] = np.stack([cosT, sinT])
    R = np.zeros((64, 64), f32)
    for f in range(64):
        p = f + 16 if (f % 32) < 16 else f - 16
        R[p, f] = 1.0
    c["c_rmat"] = R
    invr = (f32(1.0) / (f32(10000.0) ** np.linspace(0.0, 1.0, 128, dtype=f32))).astype(f32)
    tpos = np.arange(NLAT, dtype=f32)
    ang = (tpos[:, None] * invr[None, :]).astype(f32)
    cosR = np.ones((128, NT), f32)
    sinR = np.zeros((128, NT), f32)
    cosR[:, NCTX:] = np.cos(ang).astype(f32).T
    sinR[:, NCTX:] = np.sin(ang).astype(f32).T
    c["c_ropeR"] = np.stack([cosR, sinR])
    m = np.ones((128, NT), f32)
    m[:, ::16] = 0.0
    c["c_scanmask"] = m
    for C, nm in ((16, "c_tri16"), (128, "c_tri128")):
        s = np.arange(C)[:, None]
        t = np.arange(C)[None, :]
        c[nm] = np.stack([(s <= t).astype(f32), (s >= t).astype(f32)])
    j = np.arange(128, dtype=f32)
    c["c_iota"] = np.stack([np.broadcast_to(j + 1, (128, 128)), np.broadcast_to(128 - j, (128, 128))]).astype(f32)
    return c


def _prep_shared(inputs):
    f32 = np.float32
    sh = {}
    sh["mod_w"] = np.ascontiguousarray(inputs["mod_w"], f32)
    sh["mod_b"] = np.stack([_chunks(inputs["mod_b"][l], 96) for l in range(4)])
    sh["norm_gain"] = np.stack([_chunks(inputs["norm_gain"][l], 32) for l in range(4)])
    sh["even_w_in"] = np.ascontiguousarray(inputs["even_w_in"], f32)
    sh["q_a_gain"] = np.stack([_chunks(inputs["q_a_gain"][i], 12) for i in range(2)])
    sh["kv_a_gain"] = np.stack([_chunks(inputs["kv_a_gain"][i], 4) for i in range(2)])
    sh["w_uq"] = np.ascontiguousarray(inputs["w_uq"], f32)
    wk = np.asarray(inputs["w_ukv"], f32).reshape(2, 512, 16, 256)
    sh["w_ukv"] = np.ascontiguousarray(np.concatenate([wk[..., :128].reshape(2, 512, 2048), wk[..., 128:].reshape(2, 512, 2048)], axis=-1))
    qk = np.zeros((2, 128, 4), f32)
    for i in range(2):
        qg = np.asarray(inputs["q_norm_gain"][i], f32)
        kg = np.asarray(inputs["k_norm_gain"][i], f32)
        qk[i, :, 0] = qg[:128]
        qk[i, :64, 1] = qg[128:]
        qk[i, :, 2] = kg[:128]
        qk[i, :64, 3] = kg[128:]
    sh["qk_gain"] = qk
    lb = np.asarray(inputs["hg_lb"], f32)
    sh["hg_lb"] = np.ascontiguousarray(lb.reshape(2, 2, 16, 128).transpose(0, 3, 1, 2).reshape(2, 128, 32))
    sh["hg_norm_gain"] = np.asarray(inputs["hg_norm_gain"], f32).reshape(2, 128, 1).copy()
    sh["even_w_out"] = np.ascontiguousarray(inputs["even_w_out"], f32)
    sh["odd_w_in"] = np.ascontiguousarray(inputs["odd_w_in"], f32)
    rd = np.asarray(inputs["ret_decay"], f32).reshape(2, 1, 32)
    sh["ret_decay"] = np.ascontiguousarray(np.broadcast_to(rd, (2, 128, 32)))
    sh["odd_w_out"] = np.ascontiguousarray(inputs["odd_w_out"], f32)
    sh.update(_consts())
    return sh


def _prep_core(inputs, b):
    f32 = np.float32
    xT = np.empty((D, NT), f32)
    xT[:, :NCTX] = np.asarray(inputs["ctx"][b], f32).T
    xT[:, NCTX:] = np.asarray(inputs["x"][b], f32).T
    cT = np.stack([_chunks(inputs["c"][b], 32), _chunks(inputs["c_ctx"], 32)], axis=-1)
    return {"xT": xT, "cT": np.ascontiguousarray(cT)}


_NC_CACHE = {}
FUSED = True


def _prep_small(inputs):
    big = ("mod_w", "even_w_in", "w_uq", "even_w_out", "odd_w_in", "odd_w_out")
    fake = {k: v for k, v in inputs.items() if k not in big and k not in ("x", "ctx")}
    for k in big:
        fake[k] = np.zeros((2, 1, 1), np.float32)
    sh = _prep_shared(fake)
    for k in big:
        sh.pop(k)
    return sh


def _kernel_multi(inputs):
    B = inputs["x"].shape[0]
    sh = _prep_small(inputs)
    w_ukv_p = sh.pop("w_ukv")
    dummy = np.zeros((1, 128, 128), np.float32)
    cur = [_prep_core(inputs, b) for b in range(B)]
    out = np.empty((B, NLAT, D), np.float32)
    if "mod" not in _NC_CACHE:
        _NC_CACHE["mod"] = build_mod_program()
    cT5 = np.ascontiguousarray(np.stack([_chunks(inputs["c"][b], 32) for b in range(B)] + [_chunks(inputs["c_ctx"], 32)], axis=-1))
    mres = run_bass_kernel_spmd(_NC_CACHE["mod"], [{"cT5": cT5, "mod_w1": np.ascontiguousarray(inputs["mod_w"][r], np.float32),
                                                     "mod_b1": np.ascontiguousarray(sh["mod_b"][r])} for r in range(4)], core_ids=list(range(4)))
    for b in range(B):
        mv = np.empty((4, 128, 96, 2), np.float32)
        for r in range(4):
            mo = np.asarray(mres.results[r]["MODO"], np.float32)
            mv[r, :, :, 0] = mo[:, :, b]
            mv[r, :, :, 1] = mo[:, :, B]
        cur[b]["MODV"] = mv
    for l in range(4):
        key = ("single", l)
        if key not in _NC_CACHE:
            _NC_CACHE[key] = Builder(layers=(l,), single=True).build()
        nc = _NC_CACHE[key]
        i = l // 2
        lw = {"mod_w": dummy}
        for nm in ("even_w_in", "w_uq", "even_w_out", "odd_w_in", "odd_w_out"):
            need = (l % 2 == 0) == nm.startswith(("even", "w_u"))
            lw[nm] = np.ascontiguousarray(inputs[nm][i:i + 1], np.float32) if need else dummy
        lw["w_ukv"] = np.ascontiguousarray(w_ukv_p[i:i + 1]) if l % 2 == 0 else dummy
        in_maps = []
        for b in range(B):
            m = dict(sh)
            m.update(lw)
            m.update(cur[b])
            in_maps.append(m)
        res = run_bass_kernel_spmd(nc, in_maps, core_ids=list(range(B)))
        for b in range(B):
            if l < 3:
                cur[b]["xT"] = np.asarray(res.results[b]["XS"], np.float32)
            else:
                out[b] = np.asarray(res.results[b]["yT"], np.float32).T
    return out


def _kernel_fused(inputs):
    B = inputs["x"].shape[0]
    if "full" not in _NC_CACHE:
        _NC_CACHE["full"] = Builder().build()
    nc = _NC_CACHE["full"]
    sh = _prep_shared(inputs)
    in_maps = []
    for b in range(B):
        m = dict(sh)
        m.update(_prep_core(inputs, b))
        in_maps.append(m)
    res = run_bass_kernel_spmd(nc, in_maps, core_ids=list(range(B)))
    out = np.empty((B, NLAT, D), np.float32)
    for b in range(B):
        out[b] = np.asarray(res.results[b]["yT"], np.float32).T
    return out


def kernel(**inputs):
    if FUSED:
        return _kernel_fused(inputs)
    return _kernel_multi(inputs)
```

```python
from contextlib import ExitStack
import numpy as np
import concourse.bass as bass
import concourse.mybir as mybir
from concourse.bass_utils import run_bass_kernel_spmd

F32 = mybir.dt.float32
BF16 = mybir.dt.bfloat16
U32 = mybir.dt.uint32
ALU = mybir.AluOpType
AF = mybir.ActivationFunctionType

ENGS = ["pe", "act", "dve", "pool", "sp"]
MULTIBLOCK = True
EPOCH = 30000
NDMASEM = 24

D = 4096
NCTX = 256
NLAT = 4096
NT = NCTX + NLAT
EPS = 1e-6
TOKBLKS = [(0, 256)] + [(256 + 512 * i, 512) for i in range(8)]
SB_A = [(0, 1280), (1280, 1024), (2304, 1024), (3328, 1024)]

E_CQ, E_CKV, E_KPE, E_GM, E_QH, E_FF, E_FB, E_IH, E_GH, E_END = 0, 1536, 2048, 2112, 4160, 6208, 8256, 10304, 12352, 14400
O_Q, O_K, O_V, O_G, O_END = 0, 4096, 8192, 16384, 24576
R_GM, R_QH, R_FF, R_FB, R_GH = E_GM + 64, E_QH + 64, E_FF + 64, E_FB + 64, E_GH + 64


class Buf:
    __slots__ = ("name", "w", "r")

    def __init__(self, name=""):
        self.name = name
        self.w = None
        self.r = {}


class Prog:
    def __init__(self, nc):
        self.nc = nc
        self.q = {e: [] for e in ENGS}
        self.cnt = {e: 0 for e in ENGS}
        self.known = {e: {} for e in ENGS}
        self.sems = {}
        self.semmax = {}
        self.dma_i = {e: 0 for e in ENGS}
        self.dma_cnt = {}
        self.eng_obj = {"pe": nc.tensor, "act": nc.scalar, "dve": nc.vector,
                        "pool": nc.gpsimd, "sp": nc.sync}

    def sem(self, key):
        s = self.sems.get(key)
        if s is None:
            s = self.nc.alloc_semaphore(name="s_%s" % "_".join(str(k) for k in key))
            self.sems[key] = s
        return s

    def _deps(self, eng, reads, writes, acc=()):
        deps = {}

        def add(k, v):
            if deps.get(k, 0) < v:
                deps[k] = v
        for b in reads:
            if b.w is not None:
                add(*b.w)
        for b in writes:
            if b.w is not None:
                if not (b in acc and b.w[0][0] == "e" and b.w[0][1] == eng):
                    add(*b.w)
            for k, v in b.r.items():
                add(k, v)
        kn = self.known[eng]
        waits = []
        for k, v in deps.items():
            if kn.get(k, 0) < v:
                kn[k] = v
                waits.append((k, v))
        return waits

    def _mark(self, key, v, reads, writes):
        for b in reads:
            if b.r.get(key, 0) < v:
                b.r[key] = v
        for b in writes:
            b.w = (key, v)
            b.r = {}
        self.semmax[key] = v

    def op(self, eng, fns, reads=(), writes=(), acc=()):
        if not isinstance(fns, (list, tuple)):
            fns = [fns]
        waits = self._deps(eng, reads, writes, acc)
        self.cnt[eng] += 1
        n = self.cnt[eng]
        ep, v = divmod(n - 1, EPOCH)
        key = ("e", eng, ep)
        self.sem(key)
        self.q[eng].append((waits, fns, key, 1))
        self._mark(key, v + 1, reads, writes)

    def dma(self, eng, out, in_, reads=(), writes=()):
        i = self.dma_i[eng]
        self.dma_i[eng] += 1
        key = ("d", eng, i % NDMASEM)
        self.sem(key)
        c = self.dma_cnt.get(key, 0)
        waits = self._deps(eng, reads, writes)
        if c > 0 and self.known[eng].get(key, 0) < 16 * c:
            self.known[eng][key] = 16 * c
            waits.append((key, 16 * c))
        self.dma_cnt[key] = c + 1
        e = self.eng_obj[eng]
        self.q[eng].append((waits, [lambda: e.dma_start(out=out, in_=in_)], key, 16))
        self._mark(key, 16 * (c + 1), reads, writes)

    def barrier(self):
        for eng in ENGS:
            kn = self.known[eng]
            waits = []
            for k, v in self.semmax.items():
                if kn.get(k, 0) < v:
                    kn[k] = v
                    waits.append((k, v))
            if waits:
                self.q[eng].append((waits, [], None, 0))

    def emit(self):
        nc = self.nc
        sems = self.sems

        def run(eng_name, engine):
            for waits, fns, key, inc in self.q[eng_name]:
                for k, v in waits:
                    engine.wait_ge(sems[k], v)
                for j, fn in enumerate(fns):
                    ins = fn()
                    if j == len(fns) - 1:
                        ins.then_inc(sems[key], inc)
        with nc.Block() as block:
            @block.tensor
            def _(e):
                run("pe", e)

            @block.scalar
            def _(e):
                run("act", e)

            @block.vector
            def _(e):
                run("dve", e)

            @block.gpsimd
            def _(e):
                run("pool", e)

            @block.sync
            def _(e):
                run("sp", e)
        self.q = {e: [] for e in ENGS}


class Tile:
    __slots__ = ("ap", "b")

    def __init__(self, ap, b=None):
        self.ap = ap
        self.b = b if b is not None else Buf()


class Ring:
    def __init__(self, tiles):
        self.t = tiles
        self.i = 0

    def next(self):
        t = self.t[self.i % len(self.t)]
        self.i += 1
        return t


class Builder:
    def __init__(self, layers=(0, 1, 2, 3), dbg=(), stop_after=None, debug_small=False, single=False):
        self.layers = list(layers)
        self.single = single
        self.small = debug_small or single
        if single:
            dbg = set(dbg) | {"XS"}
        self.dumps = []
        self.dbg = set(dbg)
        self.stop_after = stop_after
        nc = self.nc = bass.Bass("TRN2", target_bir_lowering=False)
        self.P = Prog(nc)
        self.stack = None
        self.tcount = 0
        self.pst = nc.alloc_psum_tensor("pst", [128, 1024], BF16)
        self.PS = [Tile(nc.alloc_psum_tensor("psf%d" % i, [128, 512], F32)[:, :]) for i in range(7)]
        self.PST = Tile(self.pst[:, :])
        self.PSTd = [Tile(self.pst[:, 0:512]), Tile(self.pst[:, 512:1024])]
        self.din = {}
        self.dram_bufs = {}
        self._declare()

    def inp(self, name, shape, dt=F32):
        self.din[name] = self.nc.dram_tensor(name, list(shape), dt, kind="ExternalInput").ap()
        return self.din[name]

    def scratch(self, name, shape, dt):
        kind = "ExternalOutput" if name in self.dbg else "Internal"
        t = self.nc.dram_tensor(name, list(shape), dt, kind=kind).ap()
        self.din[name] = t
        return t

    def yt(self, r0, nrows, t0, n):
        part = r0 // 6144
        assert (r0 + nrows - 1) // 6144 == part
        rr = r0 - part * 6144
        return self.din["YT%d" % part][rr:rr + nrows, t0:t0 + n]

    def dbuf(self, *key):
        b = self.dram_bufs.get(key)
        if b is None:
            b = self.dram_bufs[key] = Buf(str(key))
        return b

    def li(self, name, idx):
        return self.din[name][0 if self.small else idx]

    def _declare(self):
        def i(name, shape):
            if self.small and name in ("mod_w",):
                shape = [1, 128, 128]
            elif self.small and name in ("even_w_in", "w_uq", "w_ukv", "even_w_out", "odd_w_in", "odd_w_out"):
                need = (self.layers[0] % 2 == 0) == name.startswith(("even", "w_u"))
                shape = [1] + list(shape[1:]) if need else [1, 128, 128]
            return self.inp(name, shape)
        i("xT", [D, NT])
        i("cT", [128, 32, 2])
        i("mod_w", [4, D, 3 * D])
        i("mod_b", [4, 128, 96])
        i("norm_gain", [4, 128, 32])
        i("even_w_in", [2, D, E_END])
        i("q_a_gain", [2, 128, 12])
        i("kv_a_gain", [2, 128, 4])
        i("w_uq", [2, 1536, 3072])
        i("w_ukv", [2, 512, 4096])
        i("qk_gain", [2, 128, 4])
        i("hg_lb", [2, 128, 32])
        i("hg_norm_gain", [2, 128, 1])
        i("even_w_out", [2, D, D])
        i("odd_w_in", [2, D, O_END])
        i("ret_decay", [2, 128, 32])
        i("odd_w_out", [2, 2 * D, D])
        i("c_ident", [128, 128])
        i("c_ropeM", [2, 64, NT])
        i("c_rmat", [64, 64])
        i("c_ropeR", [2, 128, NT])
        i("c_scanmask", [128, NT])
        i("c_tri16", [2, 16, 16])
        i("c_tri128", [2, 128, 128])
        i("c_iota", [2, 128, 128])
        self.yT = self.nc.dram_tensor("yT", [D, NLAT], F32, kind="ExternalOutput").ap()
        s = self.scratch
        s("XS", [D, NT], F32)
        for part in range(4):
            s("YT%d" % part, [6144, NT], F32)
        s("VTOK", [NT, 8192], BF16)
        s("QT", [16, 192, NT], BF16)
        s("KT", [16, 192, NT], BF16)
        s("VA", [NT, 2048], BF16)
        s("OF", [8192, NT], F32)
        s("OB", [8192, NT], F32)
        s("MT", [8192, NT], BF16)
        s("WB", [8192, D], BF16)
        if self.small:
            self.inp("MODV", [4, 128, 96, 2])
        else:
            s("MODV", [4, 128, 96, 2], F32)

    def reset(self):
        self.P.barrier()
        self.P.emit()
        if self.stack is not None:
            self.stack.close()
        self.stack = ExitStack()

    def al(self, shape, dt=F32):
        self.tcount += 1
        t = self.stack.enter_context(self.nc.sbuf_tensor("t%d" % self.tcount, list(shape), dt))
        return Tile(t[:])

    def ring(self, n, shape, dt=F32):
        return Ring([self.al(shape, dt) for _ in range(n)])

    def mm_fn(self, out, lhsT, rhs, start, stop):
        nc = self.nc
        return lambda: nc.tensor.matmul(out, lhsT=lhsT, rhs=rhs, start=start, stop=stop)

    def mm(self, out, lhsT, rhs, start=True, stop=True, reads=(), writes=(), acc=None):
        self.P.op("pe", self.mm_fn(out, lhsT, rhs, start, stop), reads, writes, acc=writes if acc is None else acc)

    def mmg(self, fns, reads, writes):
        self.P.op("pe", fns, reads, writes, acc=writes)

    def tr(self, out, in_, reads, writes):
        nc = self.nc
        ident = self.ident.ap
        self.P.op("pe", lambda: nc.tensor.transpose(out=out, in_=in_, identity=ident), list(reads) + [self.ident.b], writes, acc=writes)

    def act(self, out, in_, func, reads, writes, scale=1.0, bias=None):
        nc = self.nc
        if bias is None:
            self.P.op("act", lambda: nc.scalar.activation(out=out, in_=in_, func=func, scale=scale), reads, writes)
        else:
            self.P.op("act", lambda: nc.scalar.activation(out=out, in_=in_, func=func, scale=scale, bias=bias), reads, writes)

    def _v(self, eng):
        return self.nc.vector if eng == "dve" else self.nc.gpsimd

    def tt(self, eng, out, in0, in1, op, reads, writes):
        e = self._v(eng)
        self.P.op(eng, lambda: e.tensor_tensor(out=out, in0=in0, in1=in1, op=op), reads, writes)

    def ts(self, eng, out, in0, s1, op0, reads, writes, s2=None, op1=None):
        e = self._v(eng)
        if op1 is None:
            self.P.op(eng, lambda: e.tensor_scalar(out=out, in0=in0, scalar1=s1, scalar2=None, op0=op0), reads, writes)
        else:
            self.P.op(eng, lambda: e.tensor_scalar(out=out, in0=in0, scalar1=s1, scalar2=s2, op0=op0, op1=op1), reads, writes)

    def stt(self, eng, out, in0, scalar, in1, op0, op1, reads, writes):
        e = self._v(eng)
        self.P.op(eng, lambda: e.scalar_tensor_tensor(out=out, in0=in0, scalar=scalar, in1=in1, op0=op0, op1=op1), reads, writes)

    def cp(self, eng, out, in_, reads, writes):
        if eng == "act":
            nc = self.nc
            self.P.op("act", lambda: nc.scalar.copy(out=out, in_=in_), reads, writes)
        else:
            e = self._v(eng)
            self.P.op(eng, lambda: e.tensor_copy(out=out, in_=in_), reads, writes)

    def memset(self, eng, out, val, writes):
        e = self._v(eng)
        self.P.op(eng, lambda: e.memset(out, val), (), writes)

    def recip(self, out, in_, reads, writes):
        nc = self.nc
        self.P.op("dve", lambda: nc.vector.reciprocal(out=out, in_=in_), reads, writes)

    def rstd_from(self, dst, src_ap, src_bufs, inv_n, n):
        self.ts("dve", dst.ap[:, 0:n], src_ap, inv_n, ALU.mult, src_bufs, [dst.b], s2=EPS, op1=ALU.add)
        self.act(dst.ap[:, 0:n], dst.ap[:, 0:n], AF.Sqrt, [dst.b], [dst.b])
        self.recip(dst.ap[:, 0:n], dst.ap[:, 0:n], [dst.b], [dst.b])

    def load_consts(self):
        P = self.P
        self.ident_f = self.al([128, 128])
        self.ident = self.al([128, 128], BF16)
        self.ones = self.al([128, 128], BF16)
        P.dma("sp", self.ident_f.ap, self.din["c_ident"], writes=[self.ident_f.b])
        self.cp("dve", self.ident.ap, self.ident_f.ap, [self.ident_f.b], [self.ident.b])
        self.memset("dve", self.ones.ap, 1.0, [self.ones.b])

    def phase_mod(self, l):
        P, nc = self.P, self.nc
        self.reset()
        cT = self.al([128, 32, 2])
        sg = self.al([128, 32, 2])
        sc = self.al([128, 32, 2])
        mb = self.al([128, 96])
        mv = self.al([128, 96, 2])
        P.dma("sp", cT.ap, self.din["cT"], writes=[cT.b])
        P.dma("sp", mb.ap, self.din["mod_b"][l], writes=[mb.b])
        self.act(sg.ap, cT.ap, AF.Sigmoid, [cT.b], [sg.b])
        self.tt("dve", sc.ap, cT.ap, sg.ap, ALU.mult, [cT.b, sg.b], [sc.b])
        wr = self.ring(3, [128, 32, 256])
        ps = self.PS[0]
        psv = ps.ap[:, 0:192]
        wv = self.li("mod_w", l).rearrange("(k p) c -> p k c", p=128)
        qi = 0
        for jg in range(48):
            w = wr.next()
            for kq in range(4):
                P.dma("sp" if qi % 2 == 0 else "pool", w.ap[:, kq * 8:(kq + 1) * 8, :], wv[:, kq * 8:(kq + 1) * 8, jg * 256:(jg + 1) * 256], writes=[w.b])
                qi += 1
            for jj in range(2):
                j = jg * 2 + jj
                fns = [self.mm_fn(ps.ap[:, 2 * j:2 * j + 2], w.ap[:, k, jj * 128:(jj + 1) * 128], sc.ap[:, k, :], k == 0, k == 31) for k in range(32)]
                self.mmg(fns, [w.b, sc.b], [ps.b])
        self.tt("dve", mv.ap, psv.rearrange("p (a b) -> p a b", b=2), mb.ap.unsqueeze(2).to_broadcast([128, 96, 2]), ALU.add, [ps.b, mb.b], [mv.b])
        P.dma("sp", self.din["MODV"][l], mv.ap, reads=[mv.b], writes=[self.dbuf("MODV", l)])

    def load_mod(self, l):
        P = self.P
        mv = self.al([128, 96, 2])
        ng = self.al([128, 32])
        g1 = self.al([128, 32, 2])
        P.dma("sp", mv.ap, self.din["MODV"][l], reads=[self.dbuf("MODV", l)], writes=[mv.b])
        P.dma("sp", ng.ap, self.din["norm_gain"][l], writes=[ng.b])
        self.ts("dve", g1.ap, mv.ap[:, 32:64, :], 1.0, ALU.add, [mv.b], [g1.b])
        self.tt("dve", g1.ap, g1.ap, ng.ap.unsqueeze(2).to_broadcast([128, 32, 2]), ALU.mult, [g1.b, ng.b], [g1.b])
        return mv, g1

    def phase_inproj(self, l, xsrc, xsrc_key):
        P, nc = self.P, self.nc
        even = (l % 2 == 0)
        i = l // 2
        W = self.li("even_w_in" if even else "odd_w_in", i)
        Wv = W.rearrange("(k p) c -> p k c", p=128)
        VTOK = self.din["VTOK"]
        rshift = (lambda c: 64 if (even and c >= E_GM) else 0)
        if even:
            groups = []
            for c0 in range(E_CQ, E_KPE, 512):
                groups.append(("fm", c0, 512, False))
            groups.append(("fm", E_KPE, 64, False))
            for c0 in range(E_GM, E_QH, 512):
                groups.append(("fm", c0, 512, True))
            for c0 in range(E_QH, E_IH, 512):
                groups.append(("fm", c0, 512, False))
            for c0 in range(E_IH, E_GH, 512):
                groups.append(("tm", c0, 512, c0 - E_IH))
            for c0 in range(E_GH, E_END, 512):
                groups.append(("fm", c0, 512, True))
        else:
            groups = []
            for c0 in range(O_Q, O_V, 512):
                groups.append(("fm", c0, 512, False))
            for c0 in range(O_V, O_G, 512):
                groups.append(("tm", c0, 512, c0 - O_V))
            for c0 in range(O_G, O_END, 512):
                groups.append(("fm", c0, 512, True))
        self.reset()
        self.load_consts()
        mv, g1 = self.load_mod(l)
        hT = self.al([128, 32, 1280], BF16)
        wring = self.ring(2, [128, 32, 512], BF16)
        xin = self.ring(4, [128, 512])
        sq = self.ring(2, [128, 512], BF16)
        rstd = self.al([128, 512])
        tmp = self.ring(2, [128, 512])
        stage = self.ring(4, [128, 512])
        stage_b = self.ring(3, [128, 512], BF16)
        psr = Ring(self.PS[0:6])
        pstat = self.PS[6]
        ev = 0
        for (T0, TN) in SB_A:
            subs = [(s, n) for (s, n) in TOKBLKS if T0 <= s < T0 + TN]
            for (t0, n) in subs:
                m = 1 if t0 < NCTX else 0
                off = t0 - T0
                for k in range(32):
                    x = xin.next()
                    P.dma("sp", x.ap[:, 0:n], xsrc[k * 128:(k + 1) * 128, t0:t0 + n], reads=[self.dbuf(xsrc_key, k, t0)], writes=[x.b])
                    s_ = sq.next()
                    self.act(s_.ap[:, 0:n], x.ap[:, 0:n], AF.Square, [x.b], [s_.b])
                    self.mm(pstat.ap[:, 0:n], self.ones.ap, s_.ap[:, 0:n], k == 0, k == 31, [self.ones.b, s_.b], [pstat.b])
                self.rstd_from(rstd, pstat.ap[:, 0:n], [pstat.b], 1.0 / D, n)
                for k in range(32):
                    x = xin.next()
                    P.dma("sp", x.ap[:, 0:n], xsrc[k * 128:(k + 1) * 128, t0:t0 + n], reads=[self.dbuf(xsrc_key, k, t0)], writes=[x.b])
                    t_ = tmp.next()
                    self.tt("dve", t_.ap[:, 0:n], x.ap[:, 0:n], rstd.ap[:, 0:n], ALU.mult, [x.b, rstd.b], [t_.b])
                    self.act(hT.ap[:, k, off:off + n], t_.ap[:, 0:n], AF.Identity, [t_.b, g1.b, mv.b], [hT.b],
                             scale=g1.ap[:, k, m:m + 1], bias=mv.ap[:, k, m:m + 1])
            for (kind, c0, nc_, info) in groups:
                w = wring.next()
                for kq in range(4):
                    P.dma("pool", w.ap[:, kq * 8:(kq + 1) * 8, 0:nc_], Wv[:, kq * 8:(kq + 1) * 8, c0:c0 + nc_], writes=[w.b])
                if kind == "fm":
                    silu = info
                    for cc in range(0, nc_, 128):
                        rows = min(128, nc_ - cc)
                        for (t0, n) in subs:
                            off = t0 - T0
                            ps = psr.next()
                            fns = [self.mm_fn(ps.ap[0:rows, 0:n], w.ap[:, k, cc:cc + rows], hT.ap[:, k, off:off + n], k == 0, k == 31) for k in range(32)]
                            self.mmg(fns, [w.b, hT.b], [ps.b])
                            st = stage.next()
                            if silu:
                                self.act(st.ap[0:rows, 0:n], ps.ap[0:rows, 0:n], AF.Silu, [ps.b], [st.b])
                            else:
                                ev += 1
                                self.cp("dve" if ev % 3 else "act", st.ap[0:rows, 0:n], ps.ap[0:rows, 0:n], [ps.b], [st.b])
                            yr = c0 + cc + rshift(c0)
                            P.dma("sp", self.yt(yr, rows, t0, n), st.ap[0:rows, 0:n], reads=[st.b], writes=[self.dbuf("YT", yr, t0)])
                else:
                    vc0 = info
                    for tt_ in range(TN // 128):
                        ps = psr.next()
                        fns = [self.mm_fn(ps.ap[:, 0:512], hT.ap[:, k, tt_ * 128:(tt_ + 1) * 128], w.ap[:, k, 0:512], k == 0, k == 31) for k in range(32)]
                        self.mmg(fns, [w.b, hT.b], [ps.b])
                        st = stage_b.next()
                        ev += 1
                        self.cp("dve" if ev % 3 else "act", st.ap, ps.ap[:, 0:512], [ps.b], [st.b])
                        tg = T0 + tt_ * 128
                        P.dma("sp", VTOK[tg:tg + 128, vc0:vc0 + 512], st.ap, reads=[st.b], writes=[self.dbuf("VTOK", tg // 128, vc0)])

    def phase_mla_prep(self, i):
        P, nc = self.P, self.nc
        QT, KT, VA = self.din["QT"], self.din["KT"], self.din["VA"]
        self.reset()
        self.load_consts()
        wuq = self.al([128, 12, 3072], BF16)
        wukv = self.al([128, 4, 4096], BF16)
        Wq = self.li("w_uq", i).rearrange("(k p) c -> p k c", p=128)
        Wkv = self.li("w_ukv", i).rearrange("(k p) c -> p k c", p=128)
        for k in range(12):
            P.dma("pool", wuq.ap[:, k, :], Wq[:, k, :], writes=[wuq.b])
        for k in range(4):
            P.dma("pool", wukv.ap[:, k, :], Wkv[:, k, :], writes=[wukv.b])
        qag = self.al([128, 12])
        kvag = self.al([128, 4])
        qkg = self.al([128, 4])
        rmat = self.al([64, 64])
        P.dma("sp", qag.ap, self.din["q_a_gain"][i], writes=[qag.b])
        P.dma("sp", kvag.ap, self.din["kv_a_gain"][i], writes=[kvag.b])
        P.dma("sp", qkg.ap, self.din["qk_gain"][i], writes=[qkg.b])
        P.dma("sp", rmat.ap, self.din["c_rmat"], writes=[rmat.b])
        self.ts("dve", qkg.ap[:, 0:2], qkg.ap[:, 0:2], float(192.0 ** -0.5), ALU.mult, [qkg.b], [qkg.b])
        cq = self.al([128, 12, 512])
        cqn = self.al([128, 12, 512], BF16)
        ckv = self.al([128, 4, 512])
        ckvn = self.al([128, 4, 512], BF16)
        kpe = self.al([64, 512])
        kg = self.al([64, 512])
        kpr = self.al([64, 512])
        sqk = self.al([64, 512], BF16)
        cosT = self.al([64, 512])
        sinT = self.al([64, 512])
        sq = self.ring(3, [128, 512], BF16)
        rstd = self.al([128, 512])
        rh = self.ring(2, [128, 512])
        tmp = self.ring(4, [128, 512])
        stb = self.ring(4, [128, 512], BF16)
        psr = Ring(self.PS[0:4])
        psR = self.PS[4]
        psX = self.PS[5]
        pstat = self.PS[6]
        for (t0, n) in TOKBLKS:
            P.dma("sp", cosT.ap[:, 0:n], self.din["c_ropeM"][0, :, t0:t0 + n], writes=[cosT.b])
            P.dma("sp", sinT.ap[:, 0:n], self.din["c_ropeM"][1, :, t0:t0 + n], writes=[sinT.b])
            for src0, nk, raw, nrm, gain in ((E_CQ, 12, cq, cqn, qag), (E_CKV, 4, ckv, ckvn, kvag)):
                for k in range(nk):
                    r0 = src0 + k * 128
                    P.dma("sp", raw.ap[:, k, 0:n], self.yt(r0, 128, t0, n), reads=[self.dbuf("YT", r0, t0)], writes=[raw.b])
                for k in range(nk):
                    s_ = sq.next()
                    self.act(s_.ap[:, 0:n], raw.ap[:, k, 0:n], AF.Square, [raw.b], [s_.b])
                    self.mm(pstat.ap[:, 0:n], self.ones.ap, s_.ap[:, 0:n], k == 0, k == nk - 1, [self.ones.b, s_.b], [pstat.b])
                self.rstd_from(rstd, pstat.ap[:, 0:n], [pstat.b], 1.0 / (nk * 128), n)
                for k in range(nk):
                    t_ = tmp.next()
                    self.tt("dve", t_.ap[:, 0:n], raw.ap[:, k, 0:n], rstd.ap[:, 0:n], ALU.mult, [raw.b, rstd.b], [t_.b])
                    self.act(nrm.ap[:, k, 0:n], t_.ap[:, 0:n], AF.Identity, [t_.b, gain.b], [nrm.b], scale=gain.ap[:, k:k + 1])
            P.dma("sp", kpe.ap[:, 0:n], self.yt(E_KPE, 64, t0, n), reads=[self.dbuf("YT", E_KPE, t0)], writes=[kpe.b])
            self.act(sqk.ap[:, 0:n], kpe.ap[:, 0:n], AF.Square, [kpe.b], [sqk.b])
            self.act(kg.ap[:, 0:n], kpe.ap[:, 0:n], AF.Identity, [kpe.b, qkg.b], [kg.b], scale=qkg.ap[0:64, 3:4])
            self.mm(psR.ap[0:64, 0:n], rmat.ap, kg.ap[:, 0:n], True, True, [rmat.b, kg.b], [psR.b])
            t1 = tmp.next()
            self.tt("dve", t1.ap[0:64, 0:n], kg.ap[:, 0:n], cosT.ap[:, 0:n], ALU.mult, [kg.b, cosT.b], [t1.b])
            t2 = tmp.next()
            self.tt("dve", t2.ap[0:64, 0:n], psR.ap[0:64, 0:n], sinT.ap[:, 0:n], ALU.mult, [psR.b, sinT.b], [t2.b])
            self.tt("dve", kpr.ap[:, 0:n], t1.ap[0:64, 0:n], t2.ap[0:64, 0:n], ALU.add, [t1.b, t2.b], [kpr.b])
            for h in range(16):
                pa = psr.next()
                fns = [self.mm_fn(pa.ap[:, 0:n], wuq.ap[:, k, h * 192:h * 192 + 128], cqn.ap[:, k, 0:n], k == 0, k == 11) for k in range(12)]
                self.mmg(fns, [wuq.b, cqn.b], [pa.b])
                pb = psr.next()
                fns = [self.mm_fn(pb.ap[0:64, 0:n], wuq.ap[:, k, h * 192 + 128:h * 192 + 192], cqn.ap[:, k, 0:n], k == 0, k == 11) for k in range(12)]
                self.mmg(fns, [wuq.b, cqn.b], [pb.b])
                sa = sq.next()
                self.act(sa.ap[:, 0:n], pa.ap[:, 0:n], AF.Square, [pa.b], [sa.b])
                sb_ = sq.next()
                self.act(sb_.ap[0:64, 0:n], pb.ap[0:64, 0:n], AF.Square, [pb.b], [sb_.b])
                self.mm(pstat.ap[:, 0:n], self.ones.ap, sa.ap[:, 0:n], True, False, [self.ones.b, sa.b], [pstat.b])
                self.mm(pstat.ap[:, 0:n], self.ones.ap[0:64, :], sb_.ap[0:64, 0:n], False, True, [self.ones.b, sb_.b], [pstat.b])
                r_ = rh.next()
                self.rstd_from(r_, pstat.ap[:, 0:n], [pstat.b], 1.0 / 192, n)
                ta = tmp.next()
                self.tt("dve", ta.ap[:, 0:n], pa.ap[:, 0:n], r_.ap[:, 0:n], ALU.mult, [pa.b, r_.b], [ta.b])
                oa = stb.next()
                self.act(oa.ap[:, 0:n], ta.ap[:, 0:n], AF.Identity, [ta.b, qkg.b], [oa.b], scale=qkg.ap[:, 0:1])
                P.dma("sp", QT[h, 0:128, t0:t0 + n], oa.ap[:, 0:n], reads=[oa.b], writes=[self.dbuf("QT", h, t0)])
                tb = tmp.next()
                self.tt("dve", tb.ap[0:64, 0:n], pb.ap[0:64, 0:n], r_.ap[0:64, 0:n], ALU.mult, [pb.b, r_.b], [tb.b])
                tg = tmp.next()
                self.act(tg.ap[0:64, 0:n], tb.ap[0:64, 0:n], AF.Identity, [tb.b, qkg.b], [tg.b], scale=qkg.ap[0:64, 1:2])
                self.mm(psX.ap[0:64, 0:n], rmat.ap, tg.ap[0:64, 0:n], True, True, [rmat.b, tg.b], [psX.b])
                self.tt("dve", tb.ap[0:64, 0:n], tg.ap[0:64, 0:n], cosT.ap[:, 0:n], ALU.mult, [tg.b, cosT.b], [tb.b])
                t3 = tmp.next()
                self.tt("dve", t3.ap[0:64, 0:n], psX.ap[0:64, 0:n], sinT.ap[:, 0:n], ALU.mult, [psX.b, sinT.b], [t3.b])
                ob = stb.next()
                self.tt("dve", ob.ap[0:64, 0:n], tb.ap[0:64, 0:n], t3.ap[0:64, 0:n], ALU.add, [tb.b, t3.b], [ob.b])
                P.dma("sp", QT[h, 128:192, t0:t0 + n], ob.ap[0:64, 0:n], reads=[ob.b], writes=[self.dbuf("QT", h, t0)])
                pk = psr.next()
                fns = [self.mm_fn(pk.ap[:, 0:n], wukv.ap[:, k, h * 128:(h + 1) * 128], ckvn.ap[:, k, 0:n], k == 0, k == 3) for k in range(4)]
                self.mmg(fns, [wukv.b, ckvn.b], [pk.b])
                sk = sq.next()
                self.act(sk.ap[:, 0:n], pk.ap[:, 0:n], AF.Square, [pk.b], [sk.b])
                self.mm(pstat.ap[:, 0:n], self.ones.ap, sk.ap[:, 0:n], True, False, [self.ones.b, sk.b], [pstat.b])
                self.mm(pstat.ap[:, 0:n], self.ones.ap[0:64, :], sqk.ap[0:64, 0:n], False, True, [self.ones.b, sqk.b], [pstat.b])
                r2 = rh.next()
                self.rstd_from(r2, pstat.ap[:, 0:n], [pstat.b], 1.0 / 192, n)
                tk = tmp.next()
                self.tt("dve", tk.ap[:, 0:n], pk.ap[:, 0:n], r2.ap[:, 0:n], ALU.mult, [pk.b, r2.b], [tk.b])
                ok = stb.next()
                self.act(ok.ap[:, 0:n], tk.ap[:, 0:n], AF.Identity, [tk.b, qkg.b], [ok.b], scale=qkg.ap[:, 2:3])
                P.dma("sp", KT[h, 0:128, t0:t0 + n], ok.ap[:, 0:n], reads=[ok.b], writes=[self.dbuf("KT", h, t0)])
                okb = stb.next()
                self.tt("dve", okb.ap[0:64, 0:n], kpr.ap[:, 0:n], r2.ap[0:64, 0:n], ALU.mult, [kpr.b, r2.b], [okb.b])
                P.dma("sp", KT[h, 128:192, t0:t0 + n], okb.ap[0:64, 0:n], reads=[okb.b], writes=[self.dbuf("KT", h, t0)])
            for tt_ in range(n // 128):
                for vg in range(4):
                    ps = psr.next()
                    fns = [self.mm_fn(ps.ap[:, 0:512], ckvn.ap[:, k, tt_ * 128:(tt_ + 1) * 128], wukv.ap[:, k, 2048 + vg * 512:2048 + (vg + 1) * 512], k == 0, k == 3) for k in range(4)]
                    self.mmg(fns, [wukv.b, ckvn.b], [ps.b])
                    st = stb.next()
                    self.cp("act", st.ap, ps.ap[:, 0:512], [ps.b], [st.b])
                    tg_ = t0 + tt_ * 128
                    P.dma("sp", VA[tg_:tg_ + 128, vg * 512:(vg + 1) * 512], st.ap, reads=[st.b], writes=[self.dbuf("VA", tg_ // 128)])

    def phase_attn(self, i, with_ctx):
        P, nc = self.P, self.nc
        QT, KT, VA, MT = self.din["QT"], self.din["KT"], self.din["VA"], self.din["MT"]
        self.reset()
        self.load_consts()
        kAr = self.ring(2, [128, NT], BF16)
        kBr = self.ring(2, [64, NT], BF16)
        vr = self.ring(2, [128, 34, 128], BF16)
        qAr = self.ring(2, [128, 512], BF16)
        qBr = self.ring(2, [64, 512], BF16)
        ptr = self.ring(3, [128, 512], BF16)
        gmr = self.ring(2, [128, 512])
        rzr = self.ring(2, [128, 512])
        attr_ = self.ring(2, [128, 512])
        outr = self.ring(2, [128, 512], BF16)
        psS = Ring(self.PS[0:2])
        psO = Ring(self.PS[2:4])
        psZ = Ring(self.PS[4:6])
        allk = [self.dbuf("KT", h, t0) for h in range(16) for (t0, n) in TOKBLKS]
        for h in range(16):
            kA, kB, v = kAr.next(), kBr.next(), vr.next()
            kdeps = [self.dbuf("KT", h, t0) for (t0, n) in TOKBLKS]
            P.dma("sp", kA.ap, KT[h, 0:128, :], reads=kdeps, writes=[kA.b])
            P.dma("sp", kB.ap, KT[h, 128:192, :], reads=kdeps, writes=[kB.b])
            for half in range(2):
                P.dma("sp", v.ap[:, half * 17:(half + 1) * 17, :],
                      VA[half * 17 * 128:(half + 1) * 17 * 128, h * 128:(h + 1) * 128].rearrange("(t p) d -> p t d", p=128),
                      reads=[self.dbuf("VA", t) for t in range(half * 17, (half + 1) * 17)], writes=[v.b])
            for qi, (t0, n) in enumerate(TOKBLKS):
                if qi == 0 and not with_ctx:
                    continue
                keys = [0, 1] if qi == 0 else list(range(34))
                qA, qB = qAr.next(), qBr.next()
                P.dma("sp", qA.ap[:, 0:n], QT[h, 0:128, t0:t0 + n], reads=[self.dbuf("QT", h, t0)], writes=[qA.b])
                P.dma("sp", qB.ap[:, 0:n], QT[h, 128:192, t0:t0 + n], reads=[self.dbuf("QT", h, t0)], writes=[qB.b])
                gm = gmr.next()
                r0 = R_GM + h * 128
                P.dma("sp", gm.ap[:, 0:n], self.yt(r0, 128, t0, n), reads=[self.dbuf("YT", r0, t0)], writes=[gm.b])
                po, pz = psO.next(), psZ.next()

                def s_mm(kt):
                    ps = psS.next()
                    self.mmg([self.mm_fn(ps.ap[:, 0:n], kA.ap[:, kt * 128:(kt + 1) * 128], qA.ap[:, 0:n], True, False),
                              self.mm_fn(ps.ap[:, 0:n], kB.ap[:, kt * 128:(kt + 1) * 128], qB.ap[:, 0:n], False, True)],
                             [kA.b, kB.b, qA.b, qB.b], [ps.b])
                    return ps
                pend = s_mm(keys[0])
                for idx, kt in enumerate(keys):
                    ps = pend
                    pt = ptr.next()
                    self.act(pt.ap[:, 0:n], ps.ap[:, 0:n], AF.Exp, [ps.b], [pt.b])
                    if idx + 1 < len(keys):
                        pend = s_mm(keys[idx + 1])
                    first, last = idx == 0, idx == len(keys) - 1
                    self.mm(po.ap[:, 0:n], v.ap[:, kt, :], pt.ap[:, 0:n], first, last, [v.b, pt.b], [po.b])
                    self.mm(pz.ap[:, 0:n], self.ones.ap, pt.ap[:, 0:n], first, last, [self.ones.b, pt.b], [pz.b])
                rz = rzr.next()
                self.recip(rz.ap[:, 0:n], pz.ap[:, 0:n], [pz.b], [rz.b])
                at = attr_.next()
                self.tt("dve", at.ap[:, 0:n], po.ap[:, 0:n], rz.ap[:, 0:n], ALU.mult, [po.b, rz.b], [at.b])
                o = outr.next()
                self.tt("pool", o.ap[:, 0:n], at.ap[:, 0:n], gm.ap[:, 0:n], ALU.mult, [at.b, gm.b], [o.b])
                P.dma("sp", MT[h * 128:(h + 1) * 128, t0:t0 + n], o.ap[:, 0:n], reads=[o.b], writes=[self.dbuf("MT", h, t0)])

    def scan_chains(self, chains, C, ndk, ndv, vt, tri, row0):
        P, nc = self.P, self.nc
        nslot = 512 // (ndv * C)
        nsteps = len(chains[0]["order"])
        dv = ndv * 128

        def group_of(c):
            return Builder.group_of(c, C, ndv)
        for ch in chains:
            ch["cur"] = None
            ch["done"] = 0
            ch["pend"] = None

        def emit_o(ch, pend):
            c, am, sidx, (gid, slot, tok0, ns), po = pend
            Qt, Sbl = ch["Qt"], ch["Sb"]
            Sb = Sbl[sidx % 3]
            cs = slice(c * C, (c + 1) * C)
            fns = []
            for dvc in range(ndv):
                col = slot * (ndv * C) + dvc * C
                fns.append(self.mm_fn(po.ap[:, col:col + C], vt.ap[0:C, c, dvc * 128:(dvc + 1) * 128], am.ap[0:C, 0:C], True, False))
                for dc in range(ndk):
                    fns.append(self.mm_fn(po.ap[:, col:col + C], Sb.ap[:, dc, dvc * 128:(dvc + 1) * 128], Qt.ap[:, dc, cs], False, dc == ndk - 1))
            self.mmg(fns, [vt.b, am.b, Sb.b, Qt.b], [po.b])
            ch["done"] += 1
            if ch["done"] == ns:
                ch["done"] = 0
                st = ch["ost"].next()
                w = ns * ndv * C
                self.cp("act", st.ap[:, 0:w], po.ap[:, 0:w], [po.b], [st.b])
                for dvc in range(ndv):
                    r0 = row0 + dvc * 128
                    src = st.ap[:, 0:w].rearrange("p (s v t) -> p s v t", s=ns, v=ndv)[:, :, dvc, :]
                    dstap = ch["dst"][r0:r0 + 128, tok0:tok0 + ns * C].rearrange("p (s t) -> p s t", s=ns)
                    P.dma("sp", dstap, src, reads=[st.b], writes=[self.dbuf(ch["dstkey"], r0, tok0)])
        for step in range(nsteps + 1):
            for ch in chains:
                newp = None
                if step < nsteps:
                    c = ch["order"][step]
                    d = ch["dir"]
                    Qt, Kt, S = ch["Qt"], ch["Kt"], ch["S"]
                    cs = slice(c * C, (c + 1) * C)
                    grp = group_of(c)
                    if ch["cur"] is None or ch["cur"][0] != grp[0]:
                        ch["cur"] = (grp[0], ch["psO"].next())
                    po = ch["cur"][1]
                    pa = ch["psA"].next()
                    self.mmg([self.mm_fn(pa.ap[0:C, 0:C], Kt.ap[:, dc, cs], Qt.ap[:, dc, cs], dc == 0, dc == ndk - 1) for dc in range(ndk)],
                             [Kt.b, Qt.b], [pa.b])
                    am = ch["am"].next()
                    self.tt("dve", am.ap[0:C, 0:C], pa.ap[0:C, 0:C], tri.ap[0:C, d, :], ALU.mult, [pa.b, tri.b], [am.b])
                    kd = ch["kd"].next()
                    for dc in range(ndk):
                        kdt = ch["kdt"].next()
                        self.act(kdt.ap[:, 0:C], Kt.ap[:, dc, cs], AF.Identity, [Kt.b, ch["ELb"]], [kdt.b], scale=ch["EL"](dc, c))
                        self.tr(self.PSTd[d].ap[0:C, dc * 128:(dc + 1) * 128], kdt.ap[:, 0:C], [kdt.b], [self.PSTd[d].b])
                    self.cp("dve", kd.ap[0:C, 0:ndk * 128], self.PSTd[d].ap[0:C, 0:ndk * 128], [self.PSTd[d].b], [kd.b])
                    for dc in range(ndk):
                        pS = ch["psS"][dc]
                        self.mm(pS.ap[:, 0:dv], kd.ap[0:C, dc * 128:(dc + 1) * 128], vt.ap[0:C, c, :], True, True, [kd.b, vt.b], [pS.b])
                    newp = (c, am, step, grp, po)
                if ch["pend"] is not None:
                    emit_o(ch, ch["pend"])
                ch["pend"] = newp
                if step < nsteps:
                    for dc in range(ndk):
                        pS = ch["psS"][dc]
                        self.stt("dve", S.ap[:, dc, :], S.ap[:, dc, :], ch["EL"](dc, c), pS.ap[:, 0:dv], ALU.mult, ALU.add, [S.b, pS.b, ch["ELb"]], [S.b])
                    Sbn = ch["Sb"][(step + 1) % 3]
                    self.cp("act", Sbn.ap, S.ap, [S.b], [Sbn.b])

    @staticmethod
    def group_of(c, C, ndv):
        nslot = 512 // (ndv * C)
        ctxc = NCTX // C
        if c < ctxc:
            gs = min(nslot, ctxc)
            return (c // gs, c % gs, (c // gs) * gs * C, gs)
        g = (c - ctxc) // nslot
        return (1000 + g, (c - ctxc) % nslot, NCTX + g * nslot * C, nslot)

    @staticmethod
    def orders(C):
        nch = NT // C
        nctx = NCTX // C
        fwd = list(range(nch))
        bwd = list(range(nctx - 1, -1, -1)) + list(range(nch - 1, nctx - 1, -1))
        return fwd, bwd

    def phase_hgrn(self, i):
        P, nc = self.P, self.nc
        VTOK, OF, OB = self.din["VTOK"], self.din["OF"], self.din["OB"]
        C = 16
        NCH = NT // C
        self.reset()
        self.load_consts()
        tri = self.al([C, 2, C])
        P.dma("sp", tri.ap, self.din["c_tri16"].rearrange("d s t -> s d t"), writes=[tri.b])
        smask = self.al([128, NT])
        P.dma("sp", smask.ap, self.din["c_scanmask"], writes=[smask.b])
        lb = self.al([128, 32])
        omlb = self.al([128, 32])
        if i == 0:
            self.memset("dve", lb.ap, 0.0, [lb.b])
        else:
            l0 = self.al([128, 32])
            l1 = self.al([128, 32])
            P.dma("sp", l0.ap, self.din["hg_lb"][0], writes=[l0.b])
            P.dma("sp", l1.ap, self.din["hg_lb"][1], writes=[l1.b])
            self.tt("dve", l1.ap, l1.ap, l0.ap, ALU.subtract, [l1.b, l0.b], [l1.b])
            self.act(lb.ap, l1.ap, AF.Sigmoid, [l1.b], [lb.b])
        self.ts("dve", omlb.ap, lb.ap, -1.0, ALU.mult, [lb.b], [omlb.b], s2=1.0, op1=ALU.add)
        a1 = self.al([128, NT])
        a2 = self.al([128, NT])
        a3 = self.al([128, NT])
        QK = {d: (self.al([128, 1, NT], BF16), self.al([128, 1, NT], BF16)) for d in (0, 1)}
        EL = {d: self.al([128, NCH]) for d in (0, 1)}
        Tt = self.al([128, NCH])
        vt = self.al([C, NCH, 128], BF16)
        S = {d: self.al([128, 1, 128]) for d in (0, 1)}
        Sb = {d: [self.al([128, 1, 128], BF16) for _ in range(3)] for d in (0, 1)}
        aux = {d: dict(am=self.ring(3, [C, C], BF16), kd=self.ring(2, [C, 128], BF16), kdt=self.ring(2, [128, C], BF16),
                       ost=self.ring(2, [128, 512])) for d in (0, 1)}
        psA_tiles = [Tile(self.PS[4].ap[:, s * 32:(s + 1) * 32]) for s in range(8)]
        fwd, bwd = self.orders(C)
        e_ = nc.vector
        for h in range(16):
            rq = R_QH + h * 128
            P.dma("sp", vt.ap, VTOK[:, h * 128:(h + 1) * 128].rearrange("(c s) v -> s c v", s=C),
                  reads=[self.dbuf("VTOK", t, vc) for t in range(34) for vc in (0, 512, 1024, 1536)], writes=[vt.b])
            for d in (0, 1):
                rf = (R_FF if d == 0 else R_FB) + h * 128
                col = d * 16 + h
                Qt, Kt = QK[d]
                P.dma("sp", a1.ap, self.yt(rf, 128, 0, NT), reads=[self.dbuf("YT", rf, t0) for (t0, n) in TOKBLKS], writes=[a1.b])
                self.act(a1.ap, a1.ap, AF.Sigmoid, [a1.b], [a1.b])
                self.ts("dve", a1.ap, a1.ap, omlb.ap[:, col:col + 1], ALU.mult, [a1.b, omlb.b, lb.b], [a1.b], s2=lb.ap[:, col:col + 1], op1=ALU.add)
                self.ts("pool", a3.ap, a1.ap, -1.0, ALU.mult, [a1.b], [a3.b], s2=1.0, op1=ALU.add)
                self.act(a2.ap, a1.ap, AF.Ln, [a1.b], [a2.b])
                self.P.op("dve", (lambda o=a1.ap, m=smask.ap, x=a2.ap: e_.tensor_tensor_scan(out=o, data0=m, data1=x, initial=0.0, op0=ALU.mult, op1=ALU.add)),
                          [smask.b, a2.b], [a1.b])
                bi3 = a1.ap.rearrange("p (c s) -> p c s", s=C)
                self.cp("pool", Tt.ap, bi3[:, :, C - 1], [a1.b], [Tt.b])
                if d == 0:
                    b_, other = a1, a2
                else:
                    self.tt("dve", a2.ap, a2.ap, a1.ap, ALU.subtract, [a2.b, a1.b], [a2.b])
                    self.tt("dve", a2.ap.rearrange("p (c s) -> p c s", s=C), a2.ap.rearrange("p (c s) -> p c s", s=C),
                            Tt.ap.unsqueeze(2).to_broadcast([128, NCH, C]), ALU.add, [a2.b, Tt.b], [a2.b])
                    b_, other = a2, a1
                self.act(EL[d].ap, Tt.ap, AF.Exp, [Tt.b], [EL[d].b])
                self.act(other.ap, b_.ap, AF.Exp, [b_.b], [other.b], scale=-1.0)
                self.tt("pool", Kt.ap[:, 0, :], a3.ap, other.ap, ALU.mult, [a3.b, other.b], [Kt.b])
                self.act(other.ap, b_.ap, AF.Exp, [b_.b], [other.b])
                P.dma("sp", a3.ap, self.yt(rq, 128, 0, NT), reads=[self.dbuf("YT", rq, t0) for (t0, n) in TOKBLKS], writes=[a3.b])
                self.tt("dve", Qt.ap[:, 0, :], a3.ap, other.ap, ALU.mult, [a3.b, other.b], [Qt.b])
                self.memset("dve", S[d].ap, 0.0, [S[d].b])
                self.memset("dve", Sb[d][0].ap, 0.0, [Sb[d][0].b])
            chains = []
            for d in (0, 1):
                el = EL[d]
                chains.append(dict(Qt=QK[d][0], Kt=QK[d][1], EL=(lambda dc, c, el=el: el.ap[:, c:c + 1]), ELb=el.b,
                                   order=fwd if d == 0 else bwd, S=S[d], Sb=Sb[d], dir=d,
                                   psO=Ring(self.PS[0:2] if d == 0 else self.PS[2:4]),
                                   psA=Ring(psA_tiles[d * 4:(d + 1) * 4]), psS=[self.PS[5 + d]],
                                   dst=OF if d == 0 else OB, dstkey="OF" if d == 0 else "OB", **aux[d]))
            self.scan_chains(chains, C, 1, 1, vt, tri, h * 128)

    def phase_ret(self, i):
        P, nc = self.P, self.nc
        VTOK, OF, OB = self.din["VTOK"], self.din["OF"], self.din["OB"]
        C = 128
        NCH = NT // C
        self.reset()
        self.load_consts()
        tri = self.al([128, 2, 128])
        P.dma("sp", tri.ap, self.din["c_tri128"].rearrange("d s t -> s d t"), writes=[tri.b])
        iota = self.al([128, 2, 128])
        P.dma("sp", iota.ap, self.din["c_iota"].rearrange("d p j -> p d j"), writes=[iota.b])
        rd = self.al([128, 32])
        y = self.al([128, 32])
        lg = self.al([128, 32])
        nlg = self.al([128, 32])
        P.dma("sp", rd.ap, self.din["ret_decay"][i], writes=[rd.b])
        self.act(y.ap, rd.ap, AF.Exp, [rd.b], [y.b], scale=-1.0)
        NTERM = 10
        self.memset("dve", lg.ap, float(((-1) ** (NTERM + 1)) / NTERM), [lg.b])
        for kk in range(NTERM - 1, 0, -1):
            self.tt("dve", lg.ap, lg.ap, y.ap, ALU.mult, [lg.b, y.b], [lg.b])
            self.ts("dve", lg.ap, lg.ap, float(((-1) ** (kk + 1)) / kk), ALU.add, [lg.b], [lg.b])
        self.tt("dve", nlg.ap, lg.ap, y.ap, ALU.mult, [lg.b, y.b], [nlg.b])
        self.ts("dve", lg.ap, nlg.ap, -1.0, ALU.mult, [nlg.b], [lg.b])
        el = self.al([128, 32])
        self.act(el.ap, lg.ap, AF.Exp, [lg.b], [el.b], scale=float(C))
        tabs = self.al([128, 4, 128])
        QK = {d: (self.al([128, 2, NT], BF16), self.al([128, 2, NT], BF16)) for d in (0, 1)}
        vt = self.al([128, NCH, 512], BF16)
        S = {d: self.al([128, 2, 512]) for d in (0, 1)}
        Sb = {d: [self.al([128, 2, 512], BF16) for _ in range(3)] for d in (0, 1)}
        aux = {d: dict(am=self.ring(3, [128, 128], BF16), kd=self.ring(2, [128, 256], BF16), kdt=self.ring(2, [128, 128], BF16),
                       ost=self.ring(2, [128, 512])) for d in (0, 1)}
        x1r = self.ring(2, [128, 512])
        x2r = self.ring(2, [128, 512])
        cosr = self.ring(2, [128, 512])
        sinr = self.ring(2, [128, 512])
        tr_ = self.ring(6, [128, 512])
        fwd, bwd = self.orders(C)
        ropeR = self.din["c_ropeR"]
        for h in range(16):
            P.dma("sp", vt.ap, VTOK[:, h * 512:(h + 1) * 512].rearrange("(c s) v -> s c v", s=C),
                  reads=[self.dbuf("VTOK", t, h * 512) for t in range(34)], writes=[vt.b])
            for d in (0, 1):
                col = d * 16 + h
                self.act(tabs.ap[:, 2 * d, :], iota.ap[:, d, :], AF.Exp, [iota.b, lg.b], [tabs.b], scale=lg.ap[:, col:col + 1])
                self.act(tabs.ap[:, 2 * d + 1, :], iota.ap[:, d, :], AF.Exp, [iota.b, nlg.b], [tabs.b], scale=nlg.ap[:, col:col + 1])
                self.ts("dve", tabs.ap[:, 2 * d + 1, :], tabs.ap[:, 2 * d + 1, :], 1.0 / 16.0, ALU.mult, [tabs.b], [tabs.b])
            for which in (0, 1):
                base = (O_Q if which == 0 else O_K) + h * 256
                for (t0, n) in TOKBLKS:
                    x1, x2, cs_, sn_ = x1r.next(), x2r.next(), cosr.next(), sinr.next()
                    P.dma("sp", x1.ap[:, 0:n], self.yt(base, 128, t0, n), reads=[self.dbuf("YT", base, t0)], writes=[x1.b])
                    P.dma("sp", x2.ap[:, 0:n], self.yt(base + 128, 128, t0, n), reads=[self.dbuf("YT", base + 128, t0)], writes=[x2.b])
                    P.dma("sp", cs_.ap[:, 0:n], ropeR[0, :, t0:t0 + n], writes=[cs_.b])
                    P.dma("sp", sn_.ap[:, 0:n], ropeR[1, :, t0:t0 + n], writes=[sn_.b])
                    t1, t2, t3, t4 = tr_.next(), tr_.next(), tr_.next(), tr_.next()
                    self.tt("dve", t1.ap[:, 0:n], x1.ap[:, 0:n], cs_.ap[:, 0:n], ALU.mult, [x1.b, cs_.b], [t1.b])
                    self.tt("pool", t2.ap[:, 0:n], x2.ap[:, 0:n], sn_.ap[:, 0:n], ALU.mult, [x2.b, sn_.b], [t2.b])
                    self.tt("dve", t1.ap[:, 0:n], t1.ap[:, 0:n], t2.ap[:, 0:n], ALU.subtract, [t1.b, t2.b], [t1.b])
                    self.tt("pool", t3.ap[:, 0:n], x1.ap[:, 0:n], sn_.ap[:, 0:n], ALU.mult, [x1.b, sn_.b], [t3.b])
                    self.tt("dve", t4.ap[:, 0:n], x2.ap[:, 0:n], cs_.ap[:, 0:n], ALU.mult, [x2.b, cs_.b], [t4.b])
                    self.tt("pool", t3.ap[:, 0:n], t3.ap[:, 0:n], t4.ap[:, 0:n], ALU.add, [t3.b, t4.b], [t3.b])
                    nck = n // C
                    for d in (0, 1):
                        dstT = QK[d][which]
                        tab = tabs.ap[:, 2 * d + which, :].unsqueeze(1).to_broadcast([128, nck, C])
                        for dc, src in ((0, t1), (1, t3)):
                            self.tt("dve" if dc == 0 else "pool",
                                    dstT.ap[:, dc, t0:t0 + n].rearrange("p (c s) -> p c s", s=C),
                                    src.ap[:, 0:n].rearrange("p (c s) -> p c s", s=C), tab, ALU.mult, [src.b, tabs.b], [dstT.b])
            for d in (0, 1):
                self.memset("dve", S[d].ap, 0.0, [S[d].b])
                self.memset("dve", Sb[d][0].ap, 0.0, [Sb[d][0].b])
            chains = []
            for d in (0, 1):
                col = d * 16 + h
                chains.append(dict(Qt=QK[d][0], Kt=QK[d][1], EL=(lambda dc, c, col=col: el.ap[:, col:col + 1]), ELb=el.b,
                                   order=fwd if d == 0 else bwd, S=S[d], Sb=Sb[d], dir=d,
                                   psO=Ring([self.PS[d]]), psA=Ring([self.PS[2 + d]]), psS=[self.PS[4], self.PS[5]],
                                   dst=OF if d == 0 else OB, dstkey="OF" if d == 0 else "OB", **aux[d]))
            self.scan_chains(chains, C, 2, 4, vt, tri, h * 512)

    def phase_post(self, l):
        P, nc = self.P, self.nc
        OF, OB, MT = self.din["OF"], self.din["OB"], self.din["MT"]
        even = (l % 2 == 0)
        ndv = 1 if even else 4
        self.reset()
        self.load_consts()
        hgg = self.al([128, 1])
        if even:
            P.dma("sp", hgg.ap, self.din["hg_norm_gain"][l // 2], writes=[hgg.b])
        else:
            self.memset("dve", hgg.ap, 1.0, [hgg.b])
        Wo = self.li("even_w_out" if even else "odd_w_out", l // 2)
        for r in range(0, D if even else 2 * D, 512):
            P.dma("pool", self.din["WB"][r:r + 512, :], Wo[r:r + 512, :], writes=[self.dbuf("WB", r // 512)])
        ofr = self.ring(2, [128, 4, 512])
        obr = self.ring(2, [128, 4, 512])
        gr = self.ring(2, [128, 4, 512])
        sq = self.ring(2, [128, 512], BF16)
        rstd = self.ring(2, [128, 512])
        outr = self.ring(3, [128, 512], BF16)
        pstat = Ring(self.PS[0:2])
        Cc = 16 if even else 128
        for h in range(16):
            for (t0, n) in TOKBLKS:
                if l == 3 and t0 < NCTX:
                    continue
                of, ob, g = ofr.next(), obr.next(), gr.next()
                for dvc in range(ndv):
                    r0 = h * ndv * 128 + dvc * 128
                    gr0 = (R_GH if even else O_G) + r0
                    okeys = set()
                    for tt_ in range(t0, t0 + n, Cc):
                        okeys.add(Builder.group_of(tt_ // Cc, Cc, ndv)[2])
                    P.dma("sp", of.ap[:, dvc, 0:n], OF[r0:r0 + 128, t0:t0 + n], reads=[self.dbuf("OF", r0, k_) for k_ in okeys], writes=[of.b])
                    P.dma("sp", ob.ap[:, dvc, 0:n], OB[r0:r0 + 128, t0:t0 + n], reads=[self.dbuf("OB", r0, k_) for k_ in okeys], writes=[ob.b])
                    P.dma("sp", g.ap[:, dvc, 0:n], self.yt(gr0, 128, t0, n), reads=[self.dbuf("YT", gr0, t0)], writes=[g.b])
                self.tt("pool", of.ap[:, 0:ndv, 0:n], of.ap[:, 0:ndv, 0:n], ob.ap[:, 0:ndv, 0:n], ALU.add, [of.b, ob.b], [of.b])
                ps = pstat.next()
                for dvc in range(ndv):
                    s_ = sq.next()
                    self.act(s_.ap[:, 0:n], of.ap[:, dvc, 0:n], AF.Square, [of.b], [s_.b])
                    self.mm(ps.ap[:, 0:n], self.ones.ap, s_.ap[:, 0:n], dvc == 0, dvc == ndv - 1, [self.ones.b, s_.b], [ps.b])
                r_ = rstd.next()
                self.rstd_from(r_, ps.ap[:, 0:n], [ps.b], 1.0 / (ndv * 128), n)
                for dvc in range(ndv):
                    r0 = h * ndv * 128 + dvc * 128
                    self.tt("dve", of.ap[:, dvc, 0:n], of.ap[:, dvc, 0:n], r_.ap[:, 0:n], ALU.mult, [of.b, r_.b], [of.b])
                    o = outr.next()
                    self.stt("dve", o.ap[:, 0:n], of.ap[:, dvc, 0:n], hgg.ap[:, 0:1], g.ap[:, dvc, 0:n], ALU.mult, ALU.mult, [of.b, hgg.b, g.b], [o.b])
                    mr0 = (2048 if even else 0) + r0
                    P.dma("sp", MT[mr0:mr0 + 128, t0:t0 + n], o.ap[:, 0:n], reads=[o.b], writes=[self.dbuf("MT", mr0 // 128, t0)])

    def phase_outproj(self, l, xsrc, xsrc_key, xdst, xdst_key, lat_only):
        P, nc = self.P, self.nc
        even = (l % 2 == 0)
        i = l // 2
        MT = self.din["MT"]
        Wv = self.din["WB"].rearrange("(k p) c -> p k c", p=128)
        nk = 32 if even else 64
        GC = 512 if even else 256
        self.reset()
        self.load_consts()
        mv, g1 = self.load_mod(l)
        if even:
            SBs = SB_A
            mT = self.al([128, 32, 1280], BF16)
        else:
            SBs = TOKBLKS
            mT = self.al([128, 64, 512], BF16)
        wring = self.ring(2, [128, nk, GC], BF16)
        xr = self.ring(3, [128, 512])
        outr = self.ring(3, [128, 512])
        psr = Ring(self.PS[0:6])
        for (T0, TN) in SBs:
            subs = [(s, n) for (s, n) in TOKBLKS if T0 <= s < T0 + TN]
            if lat_only:
                subs = [(s, n) for (s, n) in subs if s >= NCTX]
            if not subs:
                continue
            for (t0, n) in subs:
                off = t0 - T0
                for kq in range(0, nk, 8):
                    P.dma("sp", mT.ap[:, kq:kq + 8, off:off + n],
                          MT[kq * 128:(kq + 8) * 128, t0:t0 + n].rearrange("(k p) t -> p k t", p=128),
                          reads=[self.dbuf("MT", kk, t0) for kk in range(kq, kq + 8)], writes=[mT.b])
            for c0 in range(0, D, GC):
                w = wring.next()
                for kq in range(0, nk, 8):
                    P.dma("act", w.ap[:, kq:kq + 8, :], Wv[:, kq:kq + 8, c0:c0 + GC],
                          reads=[self.dbuf("WB", kq // 4), self.dbuf("WB", kq // 4 + 1)], writes=[w.b])
                for cc in range(0, GC, 128):
                    j = (c0 + cc) // 128
                    for (t0, n) in subs:
                        off = t0 - T0
                        m = 1 if t0 < NCTX else 0
                        ps = psr.next()
                        fns = [self.mm_fn(ps.ap[:, 0:n], w.ap[:, k, cc:cc + 128], mT.ap[:, k, off:off + n], k == 0, k == nk - 1) for k in range(nk)]
                        self.mmg(fns, [w.b, mT.b], [ps.b])
                        x = xr.next()
                        P.dma("sp", x.ap[:, 0:n], xsrc[j * 128:(j + 1) * 128, t0:t0 + n], reads=[self.dbuf(xsrc_key, j, t0)], writes=[x.b])
                        o = outr.next()
                        self.stt("dve", o.ap[:, 0:n], ps.ap[:, 0:n], mv.ap[:, 64 + j, m:m + 1], x.ap[:, 0:n], ALU.mult, ALU.add, [ps.b, mv.b, x.b], [o.b])
                        if xdst_key == "yT":
                            dst = xdst[j * 128:(j + 1) * 128, t0 - NCTX:t0 - NCTX + n]
                        else:
                            dst = xdst[j * 128:(j + 1) * 128, t0:t0 + n]
                        P.dma("sp", dst, o.ap[:, 0:n], reads=[o.b], writes=[self.dbuf(xdst_key, j, t0)])

    def build(self):
        XS = self.din["XS"]
        xT = self.din["xT"]
        first = True
        if not self.small:
            for l in self.layers:
                self.phase_mod(l)
        for l in self.layers:
            src, skey = (xT, "xT") if first else (XS, "XS")
            first = False
            self.phase_inproj(l, src, skey)
            if self.stop_after == ("inproj", l):
                break
            if l % 2 == 0:
                self.phase_mla_prep(l // 2)
                if self.stop_after == ("mla_prep", l):
                    break
                self.phase_attn(l // 2, with_ctx=(l < 3))
                if self.stop_after == ("attn", l):
                    break
                self.phase_hgrn(l // 2)
                if self.stop_after == ("scan", l):
                    break
            else:
                self.phase_ret(l // 2)
                if self.stop_after == ("scan", l):
                    break
            self.phase_post(l)
            if self.stop_after == ("post", l):
                break
            last = (l == self.layers[-1])
            if last and l == 3:
                self.phase_outproj(l, src, skey, self.yT, "yT", True)
            else:
                self.phase_outproj(l, src, skey, XS, "XS", False)
        self.P.barrier()
        for di, (name, rs, cs) in enumerate(self.dumps):
            if name == "YT":
                src = self.yt(rs[0], rs[1] - rs[0], cs[0], cs[1] - cs[0])
            elif ":" in name:
                nm, hh = name.split(":")
                src = self.din[nm][int(hh), rs[0]:rs[1], cs[0]:cs[1]]
            else:
                src = self.din[name][rs[0]:rs[1], cs[0]:cs[1]]
            o = self.nc.dram_tensor("dump%d" % di, [rs[1] - rs[0], cs[1] - cs[0]], src.dtype, kind="ExternalOutput").ap()
            self.P.dma("sp", o, src)
        self.P.barrier()
        self.P.emit()
        if self.stack is not None:
            self.stack.close()
        return self.nc


def build_mod_program():
    nc = bass.Bass("TRN2", target_bir_lowering=False)
    P = Prog(nc)
    cT_d = nc.dram_tensor("cT5", [128, 32, 5], F32, kind="ExternalInput").ap()
    w_d = nc.dram_tensor("mod_w1", [D, 3 * D], F32, kind="ExternalInput").ap()
    b_d = nc.dram_tensor("mod_b1", [128, 96], F32, kind="ExternalInput").ap()
    o_d = nc.dram_tensor("MODO", [128, 96, 5], F32, kind="ExternalOutput").ap()
    ps = nc.alloc_psum_tensor("psm", [128, 512], F32)
    T = lambda name, shape: Tile(nc.alloc_sbuf_tensor(name, shape, F32)[:])
    cT, sg, sc, mb, mv = T("cT", [128, 32, 5]), T("sg", [128, 32, 5]), T("sc", [128, 32, 5]), T("mb", [128, 96]), T("mv", [128, 96, 5])
    wr = Ring([Tile(nc.alloc_sbuf_tensor("w%d" % j, [128, 32, 256], F32)[:]) for j in range(3)])
    pt = Tile(ps[:, :])
    P.dma("sp", cT.ap, cT_d, writes=[cT.b])
    P.dma("sp", mb.ap, b_d, writes=[mb.b])
    P.op("act", lambda: nc.scalar.activation(out=sg.ap, in_=cT.ap, func=AF.Sigmoid), [cT.b], [sg.b])
    P.op("dve", lambda: nc.vector.tensor_tensor(out=sc.ap, in0=cT.ap, in1=sg.ap, op=ALU.mult), [cT.b, sg.b], [sc.b])

    def mmf(out, lhsT, rhs, start, stop):
        return lambda: nc.tensor.matmul(out, lhsT=lhsT, rhs=rhs, start=start, stop=stop)
    wv = w_d.rearrange("(k p) c -> p k c", p=128)
    qi = 0
    for jg in range(48):
        w = wr.next()
        for kq in range(4):
            P.dma("sp" if qi % 2 == 0 else "pool", w.ap[:, kq * 8:(kq + 1) * 8, :], wv[:, kq * 8:(kq + 1) * 8, jg * 256:(jg + 1) * 256], writes=[w.b])
            qi += 1
        for jj in range(2):
            j = jg * 2 + jj
            fns = [mmf(pt.ap[:, 5 * j:5 * j + 5], w.ap[:, k, jj * 128:(jj + 1) * 128], sc.ap[:, k, :], k == 0, k == 31) for k in range(32)]
            P.op("pe", fns, [w.b, sc.b], [pt.b], acc=[pt.b])
    P.op("dve", lambda: nc.vector.tensor_tensor(out=mv.ap, in0=pt.ap[:, 0:480].rearrange("p (a b) -> p a b", b=5),
                                                in1=mb.ap.unsqueeze(2).to_broadcast([128, 96, 5]), op=ALU.add), [pt.b, mb.b], [mv.b])
    ob = Buf()
    P.dma("sp", o_d, mv.ap, reads=[mv.b], writes=[ob])
    P.barrier()
    P.emit()
    return nc


def _chunks(v, nchunk):
    return np.ascontiguousarray(np.asarray(v, np.float32).reshape(nchunk, 128).T)


def _consts():
    f32 = np.float32
    c = {}
    c["c_ident"] = np.eye(128, dtype=f32)
    rows = NLAT // 64
    row = np.repeat(np.arange(rows), 64).astype(f32)
    col = np.tile(np.arange(64), rows).astype(f32)
    half = 32
    inv = (f32(10000.0) ** (-np.arange(0, half, 2, dtype=f32) / f32(half))).astype(f32)
    cosT = np.ones((64, NT), f32)
    sinT = np.zeros((64, NT), f32)
    for g, pos in enumerate((row, col)):
        ang = (pos[:, None] * inv[None, :]).astype(f32)
        cs, sn = np.cos(ang).astype(f32).T, np.sin(ang).astype(f32).T
        cosT[g * 32:g * 32 + 16, NCTX:] = cs
        cosT[g * 32 + 16:g * 32 + 32, NCTX:] = cs
        sinT[g * 32:g * 32 + 16, NCTX:] = -sn
        sinT[g * 32 + 16:g * 32 + 32, NCTX:] = sn
    c["c_ropeM"] = np.stack([cosT, sinT])
    R = np.zeros((64, 64), f32)
    for f in range(64):
        p = f + 16 if (f % 32) < 16 else f - 16
        R[p, f] = 1.0
    c["c_rmat"] = R
    invr = (f32(1.0) / (f32(10000.0) ** np.linspace(0.0, 1.0, 128, dtype=f32))).astype(f32)
    tpos = np.arange(NLAT, dtype=f32)
    ang = (tpos[:, None] * invr[None, :]).astype(f32)
    cosR = np.ones((128, NT), f32)
    sinR = np.zeros((128, NT), f32)
    cosR[:, NCTX:] = np.cos(ang).astype(f32).T
    sinR[:, NCTX:] = np.sin(ang).astype(f32).T
    c["c_ropeR"] = np.stack([cosR, sinR])
    m = np.ones((128, NT), f32)
    m[:, ::16] = 0.0
    c["c_scanmask"] = m
    for C, nm in ((16, "c_tri16"), (128, "c_tri128")):
        s = np.arange(C)[:, None]
        t = np.arange(C)[None, :]
        c[nm] = np.stack([(s <= t).astype(f32), (s >= t).astype(f32)])
    j = np.arange(128, dtype=f32)
    c["c_iota"] = np.stack([np.broadcast_to(j + 1, (128, 128)), np.broadcast_to(128 - j, (128, 128))]).astype(f32)
    return c


def _prep_shared(inputs):
    f32 = np.float32
    sh = {}
    sh["mod_w"] = np.ascontiguousarray(inputs["mod_w"], f32)
    sh["mod_b"] = np.stack([_chunks(inputs["mod_b"][l], 96) for l in range(4)])
    sh["norm_gain"] = np.stack([_chunks(inputs["norm_gain"][l], 32) for l in range(4)])
    sh["even_w_in"] = np.ascontiguousarray(inputs["even_w_in"], f32)
    sh["q_a_gain"] = np.stack([_chunks(inputs["q_a_gain"][i], 12) for i in range(2)])
    sh["kv_a_gain"] = np.stack([_chunks(inputs["kv_a_gain"][i], 4) for i in range(2)])
    sh["w_uq"] = np.ascontiguousarray(inputs["w_uq"], f32)
    wk = np.asarray(inputs["w_ukv"], f32).reshape(2, 512, 16, 256)
    sh["w_ukv"] = np.ascontiguousarray(np.concatenate([wk[..., :128].reshape(2, 512, 2048), wk[..., 128:].reshape(2, 512, 2048)], axis=-1))
    qk = np.zeros((2, 128, 4), f32)
    for i in range(2):
        qg = np.asarray(inputs["q_norm_gain"][i], f32)
        kg = np.asarray(inputs["k_norm_gain"][i], f32)
        qk[i, :, 0] = qg[:128]
        qk[i, :64, 1] = qg[128:]
        qk[i, :, 2] = kg[:128]
        qk[i, :64, 3] = kg[128:]
    sh["qk_gain"] = qk
    lb = np.asarray(inputs["hg_lb"], f32)
    sh["hg_lb"] = np.ascontiguousarray(lb.reshape(2, 2, 16, 128).transpose(0, 3, 1, 2).reshape(2, 128, 32))
    sh["hg_norm_gain"] = np.asarray(inputs["hg_norm_gain"], f32).reshape(2, 128, 1).copy()
    sh["even_w_out"] = np.ascontiguousarray(inputs["even_w_out"], f32)
    sh["odd_w_in"] = np.ascontiguousarray(inputs["odd_w_in"], f32)
    rd = np.asarray(inputs["ret_decay"], f32).reshape(2, 1, 32)
    sh["ret_decay"] = np.ascontiguousarray(np.broadcast_to(rd, (2, 128, 32)))
    sh["odd_w_out"] = np.ascontiguousarray(inputs["odd_w_out"], f32)
    sh.update(_consts())
    return sh


def _prep_core(inputs, b):
    f32 = np.float32
    xT = np.empty((D, NT), f32)
    xT[:, :NCTX] = np.asarray(inputs["ctx"][b], f32).T
    xT[:, NCTX:] = np.asarray(inputs["x"][b], f32).T
    cT = np.stack([_chunks(inputs["c"][b], 32), _chunks(inputs["c_ctx"], 32)], axis=-1)
    return {"xT": xT, "cT": np.ascontiguousarray(cT)}


_NC_CACHE = {}
FUSED = True


def _prep_small(inputs):
    big = ("mod_w", "even_w_in", "w_uq", "even_w_out", "odd_w_in", "odd_w_out")
    fake = {k: v for k, v in inputs.items() if k not in big and k not in ("x", "ctx")}
    for k in big:
        fake[k] = np.zeros((2, 1, 1), np.float32)
    sh = _prep_shared(fake)
    for k in big:
        sh.pop(k)
    return sh


def _kernel_multi(inputs):
    B = inputs["x"].shape[0]
    sh = _prep_small(inputs)
    w_ukv_p = sh.pop("w_ukv")
    dummy = np.zeros((1, 128, 128), np.float32)
    cur = [_prep_core(inputs, b) for b in range(B)]
    out = np.empty((B, NLAT, D), np.float32)
    if "mod" not in _NC_CACHE:
        _NC_CACHE["mod"] = build_mod_program()
    cT5 = np.ascontiguousarray(np.stack([_chunks(inputs["c"][b], 32) for b in range(B)] + [_chunks(inputs["c_ctx"], 32)], axis=-1))
    mres = run_bass_kernel_spmd(_NC_CACHE["mod"], [{"cT5": cT5, "mod_w1": np.ascontiguousarray(inputs["mod_w"][r], np.float32),
                                                     "mod_b1": np.ascontiguousarray(sh["mod_b"][r])} for r in range(4)], core_ids=list(range(4)))
    for b in range(B):
        mv = np.empty((4, 128, 96, 2), np.float32)
        for r in range(4):
            mo = np.asarray(mres.results[r]["MODO"], np.float32)
            mv[r, :, :, 0] = mo[:, :, b]
            mv[r, :, :, 1] = mo[:, :, B]
        cur[b]["MODV"] = mv
    for l in range(4):
        key = ("single", l)
        if key not in _NC_CACHE:
            _NC_CACHE[key] = Builder(layers=(l,), single=True).build()
        nc = _NC_CACHE[key]
        i = l // 2
        lw = {"mod_w": dummy}
        for nm in ("even_w_in", "w_uq", "even_w_out", "odd_w_in", "odd_w_out"):
            need = (l % 2 == 0) == nm.startswith(("even", "w_u"))
            lw[nm] = np.ascontiguousarray(inputs[nm][i:i + 1], np.float32) if need else dummy
        lw["w_ukv"] = np.ascontiguousarray(w_ukv_p[i:i + 1]) if l % 2 == 0 else dummy
        in_maps = []
        for b in range(B):
            m = dict(sh)
            m.update(lw)
            m.update(cur[b])
            in_maps.append(m)
        res = run_bass_kernel_spmd(nc, in_maps, core_ids=list(range(B)))
        for b in range(B):
            if l < 3:
                cur[b]["xT"] = np.asarray(res.results[b]["XS"], np.float32)
            else:
                out[b] = np.asarray(res.results[b]["yT"], np.float32).T
    return out


def _kernel_fused(inputs):
    B = inputs["x"].shape[0]
    if "full" not in _NC_CACHE:
        _NC_CACHE["full"] = Builder().build()
    nc = _NC_CACHE["full"]
    sh = _prep_shared(inputs)
    in_maps = []
    for b in range(B):
        m = dict(sh)
        m.update(_prep_core(inputs, b))
        in_maps.append(m)
    res = run_bass_kernel_spmd(nc, in_maps, core_ids=list(range(B)))
    out = np.empty((B, NLAT, D), np.float32)
    for b in range(B):
        out[b] = np.asarray(res.results[b]["yT"], np.float32).T
    return out


def kernel(**inputs):
    if FUSED:
        return _kernel_fused(inputs)
    return _kernel_multi(inputs)
```
